# Optimizing a Trainium2 kernel written in Bass

```python
import jax
import jax.numpy as jnp
from jax import lax
import numpy as np

D_MODEL = 1024
BATCH = 8
SEQ = 2048
DEPTH = 4

CTX_LEN = 256
GRID_W = 64
HEAD_DIM = 64
ROPE_THETA = 10000.0
EPS = 1e-6
Q_BLOCK = 128
N_BRANCH = 4
BRANCH_WIDTH = 256
A_HEADS = 4
A_KV_HEADS = 2
B_HEADS = 4
B_Q_RANK = 256
B_KV_RANK = 128
B_NOPE = 64
B_ROPE = 32
B_V = 64
C_HEADS = 4
C_KV_HEADS = 2
WINDOW = 128
WIN_BLOCK = 128
POOL_WINDOWS = (2, 4, 8, 16)
POOL_GROUPS = 4
POOL_DIM = 64
D_FF = ((8 * D_MODEL + 3 * 256 - 1) // (3 * 256)) * 256
IN_SPLITS = (A_HEADS * HEAD_DIM, A_KV_HEADS * HEAD_DIM, A_KV_HEADS * HEAD_DIM,
             B_Q_RANK, B_KV_RANK, B_ROPE,
             C_HEADS * HEAD_DIM, C_KV_HEADS * HEAD_DIM, C_KV_HEADS * HEAD_DIM,
             POOL_GROUPS * POOL_DIM, N_BRANCH * D_MODEL)
D_IN = sum(IN_SPLITS)

kernel_name = 'hybrid_prefix_diffusion_block'


def rmsnorm(x, g):
    xf = x.astype(jnp.float32)
    y = xf * lax.rsqrt(jnp.mean(xf * xf, axis=-1, keepdims=True) + EPS)
    return (y * g.astype(jnp.float32)).astype(x.dtype)


def modulate(h, shift, scale):
    return h * (1.0 + scale) + shift


def axial_rope(n_tokens, rot_dim, dtype):
    rows = n_tokens // GRID_W
    row = jnp.repeat(jnp.arange(rows), GRID_W).astype(jnp.float32)
    col = jnp.tile(jnp.arange(GRID_W), rows).astype(jnp.float32)
    n_freq = rot_dim // 4
    inv = ROPE_THETA ** (-jnp.arange(n_freq, dtype=jnp.float32) / n_freq)
    ang = jnp.concatenate([row[:, None] * inv, col[:, None] * inv], axis=-1)
    return jnp.cos(ang).astype(dtype), jnp.sin(ang).astype(dtype)


def apply_rope(x, cos, sin):
    x1, x2 = jnp.split(x, 2, axis=-1)
    return jnp.concatenate([x1 * cos - x2 * sin, x1 * sin + x2 * cos], axis=-1)


def to_heads(z, n):
    b, t, _ = z.shape
    return z.reshape(b, t, n, -1).transpose(0, 2, 1, 3)


def group_q(q, n_kv):
    b, h, t, d = q.shape
    return q.reshape(b, n_kv, h // n_kv, t, d)


def merge_heads(o):
    b, hk, g, t, d = o.shape
    return o.transpose(0, 3, 1, 2, 4).reshape(b, t, hk * g * d)


def split_in(z):
    parts, off = [], 0
    for n in IN_SPLITS:
        parts.append(z[..., off:off + n])
        off += n
    return parts


def dense_attention(q, k, v, scale):
    b, hk, g, t, d = q.shape
    nb = t // Q_BLOCK
    qb = jnp.moveaxis(q.reshape(b, hk, g, nb, Q_BLOCK, d), 3, 0)

    def one_block(qi):
        s = jnp.einsum('bkgqd,bktd->bkgqt', qi, k, preferred_element_type=jnp.float32) * scale
        p = jax.nn.softmax(s, axis=-1).astype(v.dtype)
        return jnp.einsum('bkgqt,bktv->bkgqv', p, v)

    o = lax.map(one_block, qb)
    return jnp.moveaxis(o, 0, 3).reshape(b, hk, g, t, v.shape[-1])


def sink_attention(q, k, v, sink, scale):
    b, hk, g, t, d = q.shape
    s = jnp.einsum('bkgqd,bktd->bkgqt', q, k, preferred_element_type=jnp.float32) * scale
    s_sink = jnp.broadcast_to(sink.astype(jnp.float32).reshape(1, hk, g, 1, 1), s.shape[:-1] + (1,))
    p = jax.nn.softmax(jnp.concatenate([s, s_sink], axis=-1), axis=-1)[..., :-1].astype(v.dtype)
    return jnp.einsum('bkgqt,bktd->bkgqd', p, v)


def window_sink_attention(q, k, v, kc, vc, sink, scale):
    b, hk, g, t, d = q.shape
    w = WIN_BLOCK
    nb = t // w

    def band(z):
        zp = jnp.pad(z, ((0, 0), (0, 0), (w, w), (0, 0))).reshape(b, hk, nb + 2, w, z.shape[-1])
        return jnp.concatenate([zp[:, :, :-2], zp[:, :, 1:-1], zp[:, :, 2:]], axis=3)

    kb, vb = band(k), band(v)
    qb = q.reshape(b, hk, g, nb, w, d)
    s_loc = jnp.einsum('bkgnqd,bkntd->bkgnqt', qb, kb, preferred_element_type=jnp.float32) * scale
    qpos = jnp.arange(nb)[:, None] * w + jnp.arange(w)[None, :]
    kpos = (jnp.arange(nb)[:, None] - 1) * w + jnp.arange(3 * w)[None, :]
    rel = kpos[:, None, :] - qpos[:, :, None]
    valid = (jnp.abs(rel) <= WINDOW) & (kpos[:, None, :] >= 0) & (kpos[:, None, :] < t)
    s_loc = jnp.where(valid, s_loc, -1e30)
    s_ctx = jnp.einsum('bkgnqd,bkcd->bkgnqc', qb, kc, preferred_element_type=jnp.float32) * scale
    s_sink = jnp.broadcast_to(sink.astype(jnp.float32).reshape(1, hk, g, 1, 1, 1), s_loc.shape[:-1] + (1,))
    p = jax.nn.softmax(jnp.concatenate([s_loc, s_ctx, s_sink], axis=-1), axis=-1)
    p_loc = p[..., :3 * w].astype(v.dtype)
    p_ctx = p[..., 3 * w:3 * w + kc.shape[2]].astype(v.dtype)
    o = (jnp.einsum('bkgnqt,bkntd->bkgnqd', p_loc, vb)
         + jnp.einsum('bkgnqc,bkcd->bkgnqd', p_ctx, vc))
    return o.reshape(b, hk, g, t, d)


def multiscale_pool(u, w_pool, pool_scale):
    b, t, _ = u.shape
    ug = u.reshape(b, t, POOL_GROUPS, POOL_DIM)
    cs = jnp.cumsum(jnp.pad(ug.astype(jnp.float32), ((0, 0), (1, 0), (0, 0), (0, 0))), axis=1)
    pos = jnp.arange(t)
    means = []
    for gi, win in enumerate(POOL_WINDOWS):
        lo = win // 2
        hi = win - lo - 1
        start = jnp.clip(pos - lo, 0, t)
        end = jnp.clip(pos + hi + 1, 0, t)
        csg = cs[:, :, gi]
        means.append((csg[:, end] - csg[:, start]) / (end - start).astype(jnp.float32)[None, :, None])
    pooled = jnp.stack(means, axis=2).astype(u.dtype)
    mixed = jnp.einsum('btgc,gcd->btgd', pooled - ug, w_pool).reshape(b, t, -1)
    return mixed * pool_scale


def stream_heads(z, rope64, rope32, a_qn_g, a_kn_g, b_qa_g, b_kva_g, b_w_qb, b_w_kvb):
    qa, ka, va, qba, kvba, kbr, qc, kc, vc, u, gl = split_in(z)
    qa = rmsnorm(to_heads(qa, A_HEADS), a_qn_g)
    ka = rmsnorm(to_heads(ka, A_KV_HEADS), a_kn_g)
    va = to_heads(va, A_KV_HEADS)
    qb = to_heads(rmsnorm(qba, b_qa_g) @ b_w_qb, B_HEADS)
    qb_nope, qb_rope = qb[..., :B_NOPE], qb[..., B_NOPE:]
    kvb = to_heads(rmsnorm(kvba, b_kva_g) @ b_w_kvb, B_HEADS)
    kb_nope, vb = kvb[..., :B_NOPE], kvb[..., B_NOPE:]
    kbr = kbr[:, None]
    qc = to_heads(qc, C_HEADS)
    kc = to_heads(kc, C_KV_HEADS)
    vc = to_heads(vc, C_KV_HEADS)
    if rope64 is not None:
        qa, ka, qc, kc = [apply_rope(a, *rope64) for a in (qa, ka, qc, kc)]
        qb_rope = apply_rope(qb_rope, *rope32)
        kbr = apply_rope(kbr, *rope32)
    qb = jnp.concatenate([qb_nope, qb_rope], axis=-1)
    kb = jnp.concatenate([kb_nope, jnp.broadcast_to(kbr, kb_nope.shape[:-1] + (B_ROPE,))], axis=-1)
    return (group_q(qa, A_KV_HEADS), ka, va, qb[:, :, None], kb, vb,
            group_q(qc, C_KV_HEADS), kc, vc, u, gl)


def merge_branches(outs, gl, w_branch, w_out):
    o = jnp.stack(outs, axis=2)
    proj = jnp.einsum('btkc,kcd->btkd', o, w_branch)
    gates = jax.nn.sigmoid(gl.reshape(gl.shape[:-1] + (N_BRANCH, D_MODEL)))
    return jnp.sum(gates * proj, axis=2) @ w_out


def mixer_layer(hx, hc, rope64, rope32, w_in, a_qn_g, a_kn_g, b_qa_g, b_kva_g, b_w_qb, b_w_kvb,
                c_sink, d_w_pool, d_scale, w_branch, w_out, with_ctx):
    sx = stream_heads(hx @ w_in, rope64, rope32, a_qn_g, a_kn_g, b_qa_g, b_kva_g, b_w_qb, b_w_kvb)
    sc = stream_heads(hc @ w_in, None, None, a_qn_g, a_kn_g, b_qa_g, b_kva_g, b_w_qb, b_w_kvb)
    qa_x, ka_x, va_x, qb_x, kb_x, vb_x, qc_x, kc_x, vc_x, u_x, gl_x = sx
    qa_c, ka_c, va_c, qb_c, kb_c, vb_c, qc_c, kc_c, vc_c, u_c, gl_c = sc
    sc_a = HEAD_DIM ** -0.5
    sc_b = (B_NOPE + B_ROPE) ** -0.5
    cat = lambda ctx_part, lat_part: jnp.concatenate([ctx_part, lat_part], axis=2)
    o_a = dense_attention(qa_x, cat(ka_c, ka_x), cat(va_c, va_x), sc_a)
    o_b = dense_attention(qb_x, cat(kb_c, kb_x), cat(vb_c, vb_x), sc_b)
    o_c = window_sink_attention(qc_x, kc_x, vc_x, kc_c, vc_c, c_sink, sc_a)
    o_d = multiscale_pool(u_x, d_w_pool, d_scale)
    yx = merge_branches([merge_heads(o_a), merge_heads(o_b), merge_heads(o_c), o_d], gl_x, w_branch, w_out)
    if not with_ctx:
        return yx, None
    oc_a = dense_attention(qa_c, ka_c, va_c, sc_a)
    oc_b = dense_attention(qb_c, kb_c, vb_c, sc_b)
    oc_c = sink_attention(qc_c, kc_c, vc_c, c_sink, sc_a)
    oc_d = multiscale_pool(u_c, d_w_pool, d_scale)
    yc = merge_branches([merge_heads(oc_a), merge_heads(oc_b), merge_heads(oc_c), oc_d], gl_c, w_branch, w_out)
    return yx, yc


def swiglu(h, w1, w3, w2):
    return (jax.nn.silu(h @ w1) * (h @ w3)) @ w2


def setup_inputs(seed: int = 0) -> dict:
    key = jax.random.key(seed)
    ks = iter(jax.random.split(key, 32))
    f32 = jnp.float32
    L = DEPTH

    def nrm(shape, fan_in):
        return jax.random.normal(next(ks), shape, f32) * fan_in ** -0.5

    def gain(shape, noise=0.02):
        return 1.0 + noise * jax.random.normal(next(ks), shape, f32)

    return {
        'x': jax.random.normal(next(ks), (BATCH, SEQ, D_MODEL), f32),
        'c': jax.random.normal(next(ks), (BATCH, D_MODEL), f32),
        'ctx': jax.random.normal(next(ks), (BATCH, CTX_LEN, D_MODEL), f32),
        'c_ctx': jax.random.normal(next(ks), (D_MODEL,), f32),
        'ada_w': nrm((L, D_MODEL, 6 * D_MODEL), D_MODEL),
        'ada_b': 0.02 * jax.random.normal(next(ks), (L, 6 * D_MODEL), f32),
        'mix_pre_g': gain((L, D_MODEL)),
        'mix_post_g': gain((L, D_MODEL)),
        'ffn_pre_g': gain((L, D_MODEL)),
        'ffn_post_g': gain((L, D_MODEL)),
        'w_in': nrm((L, D_MODEL, D_IN), D_MODEL),
        'a_qn_g': gain((L, HEAD_DIM)),
        'a_kn_g': gain((L, HEAD_DIM)),
        'b_qa_g': gain((L, B_Q_RANK)),
        'b_kva_g': gain((L, B_KV_RANK)),
        'b_w_qb': nrm((L, B_Q_RANK, B_HEADS * (B_NOPE + B_ROPE)), B_Q_RANK),
        'b_w_kvb': nrm((L, B_KV_RANK, B_HEADS * (B_NOPE + B_V)), B_KV_RANK),
        'c_sink': jax.random.normal(next(ks), (L, C_HEADS), f32),
        'd_w_pool': nrm((L, POOL_GROUPS, POOL_DIM, POOL_DIM), POOL_DIM),
        'd_scale': gain((L, POOL_GROUPS * POOL_DIM), 0.1),
        'w_branch': nrm((L, N_BRANCH, BRANCH_WIDTH, D_MODEL), BRANCH_WIDTH),
        'w_out': nrm((L, D_MODEL, D_MODEL), D_MODEL),
        'w_ffn1': nrm((L, D_MODEL, D_FF), D_MODEL),
        'w_ffn3': nrm((L, D_MODEL, D_FF), D_MODEL),
        'w_ffn2': nrm((L, D_FF, D_MODEL), D_FF),
    }


def reference(x, c, ctx, c_ctx, ada_w, ada_b, mix_pre_g, mix_post_g, ffn_pre_g, ffn_post_g,
              w_in, a_qn_g, a_kn_g, b_qa_g, b_kva_g, b_w_qb, b_w_kvb, c_sink, d_w_pool, d_scale,
              w_branch, w_out, w_ffn1, w_ffn3, w_ffn2):
    n_tok = x.shape[1]
    rope64 = axial_rope(n_tok, HEAD_DIM, x.dtype)
    rope32 = axial_rope(n_tok, B_ROPE, x.dtype)
    xc = ctx
    for l in range(DEPTH):
        with_ctx = l < DEPTH - 1
        mod_x = (jax.nn.silu(c) @ ada_w[l] + ada_b[l])[:, None, :]
        mod_c = (jax.nn.silu(c_ctx) @ ada_w[l] + ada_b[l])[None, None, :]
        shx1, scx1, gx1, shx2, scx2, gx2 = jnp.split(mod_x, 6, axis=-1)
        shc1, scc1, gc1, shc2, scc2, gc2 = jnp.split(mod_c, 6, axis=-1)
        hx = modulate(rmsnorm(x, mix_pre_g[l]), shx1, scx1)
        hc = modulate(rmsnorm(xc, mix_pre_g[l]), shc1, scc1)
        yx, yc = mixer_layer(hx, hc, rope64, rope32, w_in[l], a_qn_g[l], a_kn_g[l], b_qa_g[l], b_kva_g[l],
                             b_w_qb[l], b_w_kvb[l], c_sink[l], d_w_pool[l], d_scale[l], w_branch[l], w_out[l],
                             with_ctx)
        x = x + gx1 * rmsnorm(yx, mix_post_g[l])
        fx = swiglu(modulate(rmsnorm(x, ffn_pre_g[l]), shx2, scx2), w_ffn1[l], w_ffn3[l], w_ffn2[l])
        x = x + gx2 * rmsnorm(fx, ffn_post_g[l])
        if with_ctx:
            xc = xc + gc1 * rmsnorm(yc, mix_post_g[l])
            fc = swiglu(modulate(rmsnorm(xc, ffn_pre_g[l]), shc2, scc2), w_ffn1[l], w_ffn3[l], w_ffn2[l])
            xc = xc + gc2 * rmsnorm(fc, ffn_post_g[l])
    return x
```

```python
import numpy as np
import ml_dtypes
import concourse.bass as bass
import concourse.mybir as mybir
from concourse.bass_utils import run_bass_kernel_spmd

F32 = mybir.dt.float32
BF16 = mybir.dt.bfloat16
ALU = mybir.AluOpType
AF = mybir.ActivationFunctionType

ENGS = ("pe", "act", "dve", "pool", "sp")
NDS = 8

D = 1024
NTOK = 2304
CTX = 256
SEQ = 2048
DFF = 2816
NJ = 22
EPS = 1e-6
GROUPS = [(0, 256, 1)] + [(256 + 512 * i, 512, 0) for i in range(4)]
POOLW = (2, 4, 8, 16)
NV = 96
TW = 544


class Op:
    __slots__ = ("eng", "fn", "dma", "deps", "signal", "semval", "idx", "dsem", "dval", "phase")

    def __init__(self, eng, fn, dma):
        self.eng = eng
        self.fn = fn
        self.dma = dma
        self.deps = []
        self.signal = False
        self.semval = 0


class Sched:
    def __init__(self, nc):
        self.nc = nc
        self.ops = {e: [] for e in ENGS}
        self.last_w = {}
        self.readers = {}
        self.ndma = {e: 0 for e in ENGS}
        self.phase = ""

    def op(self, eng, fn, reads=(), writes=(), dma=False):
        o = Op(eng, fn, dma)
        o.phase = self.phase
        deps = {}

        def add(d, kind):
            if d is None or d is o:
                return
            if not d.dma and not o.dma and d.eng == eng:
                if eng == "pe":
                    return
                if kind != "raw":
                    return
            deps[id(d)] = d

        for k in reads:
            add(self.last_w.get(k), "raw")
            if isinstance(k, tuple) and k[0] == "ps" and eng != "pe":
                for r in self.readers.get(k, ()):
                    if r.eng != eng:
                        add(r, "war")
        for k in writes:
            add(self.last_w.get(k), "waw")
            for r in self.readers.get(k, ()):
                add(r, "war")
        for k in reads:
            lst = self.readers.setdefault(k, [])
            if not dma:
                lst[:] = [r for r in lst if r.dma or r.eng != eng]
            lst.append(o)
        for k in writes:
            self.last_w[k] = o
            self.readers[k] = []
        o.deps = list(deps.values())
        if dma:
            o.idx = self.ndma[eng]
            self.ndma[eng] += 1
        self.ops[eng].append(o)
        return o

    def emit(self):
        nc = self.nc
        used = [e for e in ENGS if self.ops[e]]
        for e in used:
            for o in self.ops[e]:
                for d in o.deps:
                    d.signal = True
        csem = {e: nc.alloc_semaphore(f"c_{e}") for e in used}
        dsem = {e: [nc.alloc_semaphore(f"d_{e}_{i}") for i in range(NDS)] for e in used if self.ndma[e]}
        for e in used:
            cnt = 0
            for o in self.ops[e]:
                if o.dma:
                    o.dsem = dsem[e][o.idx % NDS]
                    o.dval = 16 * (o.idx // NDS + 1)
                elif o.signal:
                    cnt += 1
                    o.semval = cnt
        with nc.Block() as block:
            for e in used:
                deco = {"pe": block.tensor, "act": block.scalar, "dve": block.vector,
                        "pool": block.gpsimd, "sp": block.sync}[e]
                ops = self.ops[e]

                def body(engine, ops=ops, e=e):
                    waited = {}

                    def wait(sem, val):
                        key = id(sem)
                        if waited.get(key, 0) >= val:
                            return
                        waited[key] = val
                        engine.wait_ge(sem, val)

                    dmas = []
                    for o in ops:
                        for d in o.deps:
                            if d.dma:
                                wait(d.dsem, d.dval)
                            else:
                                wait(csem[d.eng], d.semval)
                        if o.dma:
                            if o.idx >= NDS:
                                wait(o.dsem, o.dval - 16)
                            o.fn().then_inc(o.dsem, 16)
                            dmas.append(o)
                        else:
                            ins = o.fn()
                            if o.signal:
                                ins.then_inc(csem[e], 1)
                    for o in dmas[-NDS:]:
                        wait(o.dsem, o.dval)

                deco(body)


class StopBuild(Exception):
    pass


def build_program(depth=4, dbg=False, emit=True):
    nc = bass.Bass("TRN2", target_bir_lowering=False)
    S = Sched(nc)

    def dram(name, shape, dt=F32, out=False):
        return nc.dram_tensor(name, list(shape), dt, kind="ExternalOutput" if out else "ExternalInput").ap()

    x_d = dram("x", [SEQ, D])
    ctx_d = dram("ctx", [CTX, D])
    cT_d = dram("cT", [128, 16])
    vec_d = dram("vec", [128, 4 * NV])
    constb_d = dram("constb", [128, 640], BF16)
    tab_d = dram("tab", [6, 128, NTOK], BF16)
    ada_d = dram("ada", [4, D, 6 * D])
    wp1_d = dram("wp1", [4, D, 1344])
    wp2_d = dram("wp2", [4, D, 5632])
    wbp_d = dram("wbp", [4, 8, D, 128])
    wo_d = dram("wo", [4, D, D])
    w13_d = dram("w13", [4, D, NJ * 256])
    w2_d = dram("w2", [4, DFF, D])
    smw_d = dram("smw", [4, 128, 2304])
    out_d = dram("out", [SEQ, D], out=True)

    sb = nc.alloc_sbuf_tensor
    X = sb("X", [128, 18, D], F32)
    KTA = sb("KTA", [128, NTOK], BF16)
    KTC = sb("KTC", [128, NTOK], BF16)
    VAC = sb("VAC", [128, 18, 256], BF16)
    KTB = sb("KTB", [96, 4, NTOK], BF16)
    VB = sb("VB", [128, 18, 256], BF16)
    HT = sb("HT", [128, 8, 512], BF16)
    R1 = sb("R1", [128, NJ * 512], BF16)
    GG = [sb(f"GG{i}", [128, D], BF16) for i in range(2)]
    TABg = sb("TABg", [128, 6, 512], BF16)
    WS = [sb(f"WS{i}", [128, 4096], BF16) for i in range(3)]
    WBS = [sb(f"WBS{i}", [128, 8, 128], BF16) for i in range(2)]
    SMW = sb("SMW", [128, 2304], BF16)
    VEC = sb("VEC", [128, 4, NV], F32)
    CONSTB = sb("CONSTB", [128, 640], BF16)
    MODT = sb("MODT", [128, 48, 2], F32)
    PRM = sb("PRM", [128, 4, 8, 2], F32)
    ES = sb("ES", [128, 2], F32)
    cT = sb("cTs", [128, 16], F32)
    scb = sb("scb", [128, 8, 2], BF16)
    XS = sb("XS", [128, D], BF16)
    T = [sb(f"T{i}", [128, TW], F32) for i in range(4)]
    UE = sb("UE", [128, 2, 528], BF16)
    HALO = sb("HALO", [128, 2, 5, 16], BF16)
    DIAG = [sb(f"DIAG{i}", [128, 128], BF16) for i in range(2)]
    SS = sb("SS", [128, 4], F32)
    LNV = sb("LNV", [128, 4], F32)
    RSTD = sb("RSTD", [128, 4], F32)
    RSTDALL = sb("RSTDALL", [128, 18], F32)
    EPSC = sb("EPSC", [128, 1], F32)
    FSC = sb("FSC", [128, 2], F32)
    PSALL = nc.alloc_psum_tensor("PSALL", [128, 4096], F32)

    ident = CONSTB[:, 0:128]
    ones = CONSTB[:, 128:256]
    bd = CONSTB[:, 256:384]
    maskL = CONSTB[:, 384:512]
    maskU = CONSTB[:, 512:640]

    def chk(n):
        if dbg == n:
            raise StopBuild()

    def PS(b):
        return PSALL[:, b * 512:(b + 1) * 512]

    TP2 = PSALL[:, 0:2048].bitcast(BF16)
    TP4 = TP2.rearrange("p (c i x) -> p c i x", c=8, i=4)

    QTA = R1[:, 0:1024].rearrange("p (a n) -> p a n", a=2)
    QTC = R1[:, 1024:2048].rearrange("p (a n) -> p a n", a=2)
    QTB = R1[:, 2048:4096].rearrange("p (a n) -> p a n", a=4)
    PT = [R1[:, 4096 + i * 512: 4096 + (i + 1) * 512] for i in range(4)]
    OTg = R1[:, 6144:10240].rearrange("p (a n) -> p a n", a=8)
    sTg = R1[:, 0:4096].rearrange("p (a n) -> p a n", a=8)
    gT = R1[:, 0:NJ * 512].rearrange("p (a n) -> p a n", a=NJ)

    def Tb(i):
        return T[i][:].bitcast(BF16)

    def mm(out, lhsT, rhs, start, stop, r, w):
        S.op("pe", lambda: nc.tensor.matmul(out, lhsT=lhsT, rhs=rhs, start=start, stop=stop), reads=r, writes=w)

    def tr(out, in_, r, w):
        S.op("pe", lambda: nc.tensor.transpose(out, in_, ident), reads=list(r) + ["const"], writes=w)

    def act(out, in_, func, r, w, scale=None, bias=None, accum_out=None):
        kw = {}
        if scale is not None:
            kw["scale"] = scale
        if bias is not None:
            kw["bias"] = bias
        if accum_out is not None:
            kw["accum_out"] = accum_out
        S.op("act", lambda: nc.scalar.activation(out=out, in_=in_, func=func, **kw), reads=r, writes=w)

    def tt(out, in0, in1, op, r, w, eng="dve"):
        if eng == "pool":
            S.op("pool", lambda: nc.gpsimd.tensor_tensor(out=out, in0=in0, in1=in1, op=op), reads=r, writes=w)
        else:
            S.op("dve", lambda: nc.vector.tensor_tensor(out=out, in0=in0, in1=in1, op=op), reads=r, writes=w)

    def ts(out, in0, s1, s2, op0, op1, r, w):
        if op1 is None:
            S.op("dve", lambda: nc.vector.tensor_scalar(out=out, in0=in0, scalar1=s1, scalar2=None, op0=op0),
                 reads=r, writes=w)
        else:
            S.op("dve", lambda: nc.vector.tensor_scalar(out=out, in0=in0, scalar1=s1, scalar2=s2, op0=op0, op1=op1),
                 reads=r, writes=w)

    def stt(out, in0, scalar, in1, op0, op1, r, w):
        S.op("dve", lambda: nc.vector.scalar_tensor_tensor(out=out, in0=in0, scalar=scalar, in1=in1, op0=op0, op1=op1),
             reads=r, writes=w)

    def cp(out, in_, r, w):
        S.op("dve", lambda: nc.vector.tensor_copy(out=out, in_=in_), reads=r, writes=w)

    def recip(out, in_, r, w):
        S.op("dve", lambda: nc.vector.reciprocal(out=out, in_=in_), reads=r, writes=w)

    def memset(ap, val, w):
        S.op("dve", lambda: nc.vector.memset(ap, val), writes=w)

    def dma_sp(out, in_, r, w):
        S.op("sp", lambda: nc.sync.dma_start(out=out, in_=in_), reads=r, writes=w, dma=True)

    def fence(keys):
        S.op("dve", lambda: nc.vector.memset(FSC[:, 0:1], 0.0), writes=list(keys) + ["fsc"])

    wctr = [0]

    def wload(src_ap, shape3):
        i = wctr[0] % 3
        wctr[0] += 1
        k, n = shape3
        dst = WS[i][:, 0:k * n].rearrange("p (k n) -> p k n", k=k)
        src = src_ap.rearrange("(k p) n -> p k n", p=128)
        key = ("ws", i)
        S.op("pool", lambda: nc.gpsimd.dma_start(out=dst, in_=src), writes=[key], dma=True)
        return dst, key

    wbctr = [0]

    def wbload(src_ap):
        i = wbctr[0] % 2
        wbctr[0] += 1
        dst = WBS[i][:]
        src = src_ap.rearrange("(k p) n -> p k n", p=128)
        key = ("wbs", i)
        S.op("pool", lambda: nc.gpsimd.dma_start(out=dst, in_=src), writes=[key], dma=True)
        return dst, key

    bankctr = [0]

    def nb(lo=0, n=8):
        b = lo + bankctr[0] % n
        bankctr[0] += 1
        return b

    def gkey(name, kt):
        return (name, 0 if kt < 2 else 1 + (kt - 2) // 4)

    dma_sp(CONSTB[:], constb_d, [], ["const"])
    dma_sp(VEC[:].rearrange("p l v -> p (l v)"), vec_d, [], ["VEC"])
    dma_sp(cT[:], cT_d, [], ["cT"])
    dma_sp(X[:, 0:2, :], ctx_d.rearrange("(t p) d -> p t d", p=128), [], [("X", 0), ("X", 1)])
    for i in range(4):
        dma_sp(X[:, 2 + 4 * i: 6 + 4 * i, :], x_d[512 * i:512 * (i + 1), :].rearrange("(t p) d -> p t d", p=128),
               [], [("X", 2 + 4 * i + k) for k in range(4)])
    memset(EPSC[:], EPS, ["EPSC"])
    act(scb[:].rearrange("p k s -> p (k s)"), cT[:], AF.Silu, ["cT"], ["scb"])

    def rsqrt_small(n, scale):
        act(LNV[:, 0:n], SS[:, 0:n], AF.Ln, ["SS", "EPSC"], ["LNV"], scale=scale, bias=EPSC[:, 0:1])
        act(RSTD[:, 0:n], LNV[:, 0:n], AF.Exp, ["LNV"], ["RSTD"], scale=-0.5)

    def rsqrt_big(out, in_, n, scale, r, w):
        act(out, in_, AF.Ln, list(r) + ["EPSC"], w, scale=scale, bias=EPSC[:, 0:1])
        act(out, out, AF.Exp, w, w, scale=-0.5)

    def p0(l):
        S.phase = "p0"
        modps = PS(6)[:, 0:96].rearrange("p (c s) -> p c s", s=2)
        for pc in range(12):
            w, wk = wload(ada_d[l][:, pc * 512:(pc + 1) * 512], (8, 512))
            for cc in range(4):
                ch = pc * 4 + cc
                for kc in range(8):
                    mm(modps[:, ch, :], w[:, kc, cc * 128:(cc + 1) * 128], scb[:, kc, :], kc == 0, kc == 7,
                       [wk, "scb"], [("ps", 6)])
        for s in range(2):
            tt(MODT[:, :, s], modps[:, :, s], VEC[:, l, 0:48], ALU.add, [("ps", 6), "VEC"], ["MODT"])
        for s in range(2):
            stt(PRM[:, 0, :, s], MODT[:, 8:16, s], 1.0, VEC[:, l, 48:56], ALU.add, ALU.mult, ["MODT", "VEC"], ["PRM"])
            stt(PRM[:, 1, :, s], MODT[:, 32:40, s], 1.0, VEC[:, l, 56:64], ALU.add, ALU.mult, ["MODT", "VEC"], ["PRM"])
            tt(PRM[:, 2, :, s], MODT[:, 16:24, s], VEC[:, l, 64:72], ALU.mult, ["MODT", "VEC"], ["PRM"])
            tt(PRM[:, 3, :, s], MODT[:, 40:48, s], VEC[:, l, 72:80], ALU.mult, ["MODT", "VEC"], ["PRM"])
        act(ES[:], VEC[:, l, 87:89], AF.Exp, ["VEC"], ["ES"])
        S.op("pool", lambda: nc.gpsimd.dma_start(out=SMW[:], in_=smw_d[l]), writes=["SMW"], dma=True)

    def gen_gg(which, s):
        S.phase = "gg"
        for c in range(8):
            dg = DIAG[c % 2]
            ts(dg[:], ident, PRM[:, 2 + which, c, s:s + 1], None, ALU.mult, None, ["PRM", "const"], [("diag", c % 2)])
            b = 4 + c // 4
            mm(PS(b)[:, (c % 4) * 128:(c % 4 + 1) * 128], ones, dg[:], True, True,
               [("diag", c % 2), "const"], [("ps", b)])
        for hh in range(2):
            act(GG[which][:, hh * 512:(hh + 1) * 512], PS(4 + hh), AF.Copy, [("ps", 4 + hh)], [("GG", which)])

    def norm_to_T(gi, which, mode="plain"):
        t0, NT, s = GROUPS[gi]
        ntt = NT // 128
        tile0 = t0 // 128
        S.phase = "norm"
        if mode != "cached":
            for i in range(ntt):
                if i % 2 == 0:
                    act(Tb(3)[:, 0:1024], X[:, tile0 + i, :], AF.Square, [("X", tile0 + i)], ["T3", "SS"],
                        accum_out=SS[:, i:i + 1])
                else:
                    xin = X[:, tile0 + i, :]
                    S.op("dve", lambda xin=xin, i=i: nc.vector.scalar_tensor_tensor(
                        out=Tb(2)[:, 0:1024], in0=xin, scalar=1.0, in1=xin, op0=ALU.mult, op1=ALU.mult,
                        accum_out=SS[:, i:i + 1]), reads=[("X", tile0 + i)], writes=["T2", "SS"])
            if mode == "store":
                act(LNV[:, 0:ntt], SS[:, 0:ntt], AF.Ln, ["SS", "EPSC"], ["LNV"], scale=1.0 / D, bias=EPSC[:, 0:1])
                act(RSTDALL[:, tile0:tile0 + ntt], LNV[:, 0:ntt], AF.Exp, ["LNV"], [("RSTDALL", gi)], scale=-0.5)
            else:
                rsqrt_small(ntt, 1.0 / D)
        if mode == "plain":
            rs, rk = RSTD, "RSTD"
            roff = 0
        else:
            rs, rk = RSTDALL, ("RSTDALL", gi)
            roff = tile0
        xsb = [(XS[:], "XS"), (Tb(0)[:, 0:1024], "T0")]
        for i in range(ntt):
            xb, xk = xsb[i % 2]
            ts(xb, X[:, tile0 + i, :], rs[:, roff + i:roff + i + 1], None, ALU.mult, None, [("X", tile0 + i), rk], [xk])
            for c in range(8):
                tr(TP4[:, c, i, :], xb[:, c * 128:(c + 1) * 128], [xk], [("ps", c // 2)])
        boff = 0 if which == 0 else 24
        for c in range(8):
            if c in (0, 1, 4, 5):
                act(HT[:, c, 0:NT], TP2[:, c * 512:c * 512 + NT], AF.Identity, [("ps", c // 2), "PRM", "MODT"], ["HT"],
                    scale=PRM[:, which, c, s:s + 1], bias=MODT[:, boff + c, s:s + 1])
            else:
                ts(HT[:, c, 0:NT], TP2[:, c * 512:c * 512 + NT], PRM[:, which, c, s:s + 1], MODT[:, boff + c, s:s + 1],
                   ALU.mult, ALU.add, [("ps", c // 2), "PRM", "MODT"], ["HT"])

    def load_tabs(gi):
        t0, NT, s = GROUPS[gi]
        dma_sp(TABg[:, :, 0:NT], tab_d[:, :, t0:t0 + NT].rearrange("a p t -> p a t"), [], ["TAB"])

    def proj(bank, wsl, wk, NT, M=128):
        for kc in range(8):
            mm(PS(bank)[0:M, 0:NT], wsl[:, kc, :], HT[:, kc, 0:NT], kc == 0, kc == 7, [wk, "HT"], [("ps", bank)])

    def rope_plain(dst, dkeys, bq, bqs, NT, rows=128, cosi=0):
        a = Tb(3)[0:rows, 0:NT]
        b = Tb(3)[0:rows, 544:544 + NT]
        tt(a, PS(bq)[0:rows, 0:NT], TABg[0:rows, cosi, 0:NT], ALU.mult, [("ps", bq), "TAB"], ["T3"])
        tt(b, PS(bqs)[0:rows, 0:NT], TABg[0:rows, cosi + 1, 0:NT], ALU.mult, [("ps", bqs), "TAB"], ["T3"])
        return a, b

    def normrope(dst, dkeys, bq, bqs, NT, l, gcol, extra_r=()):
        sq = XS[:, 0:NT]
        act(sq, PS(bq)[:, 0:NT], AF.Square, [("ps", bq)], ["XS"])
        chk(331)
        bs = nb()
        mm(PS(bs)[:, 0:NT], bd, sq, True, True, ["XS", "const"], [("ps", bs)])
        chk(332)
        rs = T[2][:, 0:NT]
        rsqrt_big(rs, PS(bs)[:, 0:NT], NT, 1.0 / 64, [("ps", bs)], ["T2"])
        chk(333)
        stt(T[0][:, 0:NT], PS(bq)[:, 0:NT], VEC[:, l, gcol:gcol + 1], TABg[:, 0, 0:NT], ALU.mult, ALU.mult,
            [("ps", bq), "VEC", "TAB"], ["T0"])
        chk(334)
        stt(T[1][:, 0:NT], PS(bqs)[:, 0:NT], VEC[:, l, gcol + 1:gcol + 2], TABg[:, 1, 0:NT], ALU.mult, ALU.mult,
            [("ps", bqs), "VEC", "TAB"], ["T1"])
        tt(T[0][:, 0:NT], T[0][:, 0:NT], T[1][:, 0:NT], ALU.add, ["T0", "T1"], ["T0"])
        tt(dst, T[0][:, 0:NT], rs, ALU.mult, ["T0", "T2"] + list(extra_r), dkeys)

    def pass1(l, gi):
        t0, NT, s = GROUPS[gi]
        ntt = NT // 128
        tile0 = t0 // 128
        norm_to_T(gi, 0, "store")
        chk(31)
        load_tabs(gi)
        S.phase = "p1proj"
        gk = gi
        wa, wk = wload(wp1_d[l][:, 0:512], (8, 512))
        b1, b2 = nb(), nb()
        proj(b1, wa[:, :, 0:128], wk, NT)
        proj(b2, wa[:, :, 128:256], wk, NT)
        chk(32)
        normrope(KTA[:, t0:t0 + NT], [("KTA", gk)], b1, b2, NT, l, 82)
        chk(33)
        b1, b2 = nb(), nb()
        proj(b1, wa[:, :, 256:384], wk, NT)
        proj(b2, wa[:, :, 384:512], wk, NT)
        a, b = rope_plain(None, None, b1, b2, NT)
        tt(KTC[:, t0:t0 + NT], a, b, ALU.add, ["T3"], [("KTC", gk)])
        chk(34)
        wb, wk = wload(wp1_d[l][:, 512:832], (8, 320))
        b1 = nb()
        proj(b1, wb[:, :, 0:128], wk, NT)
        sq = XS[:, 0:NT]
        act(sq, PS(b1)[:, 0:NT], AF.Square, [("ps", b1)], ["XS"])
        b2 = nb()
        mm(PS(b2)[:, 0:NT], ones, sq, True, True, ["XS", "const"], [("ps", b2)])
        rsqrt_big(T[2][:, 0:NT], PS(b2)[:, 0:NT], NT, 1.0 / 128, [("ps", b2)], ["T2"])
        kvn = Tb(0)[:, 0:NT]
        stt(kvn, PS(b1)[:, 0:NT], VEC[:, l, 86:87], T[2][:, 0:NT], ALU.mult, ALU.mult,
            [("ps", b1), "VEC", "T2"], ["T0"])
        for h in range(4):
            bh = nb()
            mm(PS(bh)[0:64, 0:NT], SMW[:, 1536 + h * 64:1536 + (h + 1) * 64], kvn, True, True, ["SMW", "T0"], [("ps", bh)])
            act(KTB[0:64, h, t0:t0 + NT], PS(bh)[0:64, 0:NT], AF.Copy, [("ps", bh)], [("KTB", gk)])
        for i in range(ntt):
            bv = nb()
            mm(PS(bv)[:, 0:256], kvn[:, i * 128:(i + 1) * 128], SMW[:, 1792:2048], True, True, ["SMW", "T0"], [("ps", bv)])
            cp(VB[:, tile0 + i, :], PS(bv)[:, 0:256], [("ps", bv)], [("VB", gk)])
        b1, b2 = nb(), nb()
        proj(b1, wb[:, :, 128:224], wk, NT, M=96)
        proj(b2, wb[:, :, 224:320], wk, NT, M=96)
        a, b = rope_plain(None, None, b1, b2, NT, rows=96, cosi=2)
        for h in range(4):
            tt(KTB[64:96, h, t0:t0 + NT], a[64:96, :], b[64:96, :], ALU.add, ["T3"], [("KTB", gk)])
        chk(35)
        wc, wk = wload(wp1_d[l][:, 832:1344], (8, 512))
        for i in range(ntt):
            bv = nb()
            for kc in range(8):
                mm(PS(bv)[:, 0:256], HT[:, kc, i * 128:(i + 1) * 128], wc[:, kc, 0:256], kc == 0, kc == 7,
                   [wk, "HT"], [("ps", bv)])
            act(VAC[:, tile0 + i, :], PS(bv)[:, 0:256], AF.Copy, [("ps", bv)], [("VAC", gk)])
        bu = nb()
        for c in range(2):
            for e, c0 in enumerate((0, NT - 8)):
                for kc in range(8):
                    mm(PS(bu)[:, c * 16 + e * 8:c * 16 + e * 8 + 8], wc[:, kc, 256 + c * 128:256 + (c + 1) * 128],
                       HT[:, kc, c0:c0 + 8], kc == 0, kc == 7, [wk, "HT"], [("ps", bu)])
        for c in range(2):
            cp(HALO[:, c, gi, :], PS(bu)[:, c * 16:(c + 1) * 16], [("ps", bu)], [("HALO", gi)])

    sctr = [0]
    pctr = [0]
    octr = [0]

    def attention(NT, qa, qb, ka, kb_, va, vb, rows, keylist, scale, otc, esink_col, qkeys, kkeyname, vkeyname):
        ob = 4
        O = PS(ob)
        Z = PS(ob + 1)
        n = len(keylist)
        ptsl = {}
        accv = Tb(0)[:, 0:1024].rearrange("p (a n) -> p a n", a=2)
        n_odd = n // 2
        last_even = ((n - 1) // 2) * 2

        def qk(idx):
            kt, c0, c1, masks = keylist[idx]
            pts = []
            for q_, k_ in ((qa, ka), (qb, kb_)):
                sbk = sctr[0] % 4
                sctr[0] += 1
                pi = pctr[0] % 4
                pctr[0] += 1
                mm(PS(sbk)[:, c0:c1], k_(kt), q_[:, c0:c1], True, True, qkeys + [gkey(kkeyname, kt), "Ra"], [("ps", sbk)])
                act(PT[pi][:, c0:c1], PS(sbk)[:, c0:c1], AF.Exp, [("ps", sbk), "Ra"], [("PT", pi)], scale=scale)
                for (m0, m1, mk) in masks:
                    tt(PT[pi][:, m0:m1], PT[pi][:, m0:m1], mk, ALU.mult, [("PT", pi), "const", "Ra"], [("PT", pi)])
                pts.append(pi)
            ptsl[idx] = pts

        def pv(idx):
            kt, c0, c1, masks = keylist[idx]
            first = idx == 0
            last = idx == n - 1
            pts = ptsl[idx]
            for hh, v_ in enumerate((va, vb)):
                pi = pts[hh]
                mm(O[hh * 64:(hh + 1) * 64, c0:c1], v_(kt), PT[pi][:, c0:c1], first, last,
                   [("PT", pi), gkey(vkeyname, kt), "Ra"], [("ps", ob)])
            for hh in range(2):
                pi = pts[hh]
                mm(Z[hh * 64:(hh + 1) * 64, c0:c1], ones[:, 0:64], PT[pi][:, c0:c1], first, last,
                   [("PT", pi), "const", "Ra"], [("ps", ob + 1)])

        qk(0)
        for idx in range(n):
            if idx + 1 < n:
                qk(idx + 1)
            pv(idx)
            yield
        rz = T[2][:, 0:NT]
        if esink_col is not None:
            ts(rz, Z[:, 0:NT], ES[:, esink_col:esink_col + 1], None, ALU.add, None, [("ps", ob + 1), "ES"], ["T2"])
            recip(rz, rz, ["T2"], ["T2"])
        else:
            recip(rz, Z[:, 0:NT], [("ps", ob + 1)], ["T2"])
        tt(OTg[:, otc, 0:NT], O[:, 0:NT], rz, ALU.mult, [("ps", ob), "T2", "Rb"], [("OT", otc)])
        yield

    def post_norm(gi, which):
        t0, NT, s = GROUPS[gi]
        ntt = NT // 128
        tile0 = t0 // 128
        for i in range(ntt):
            yv = PSALL[:, 2 * i * 512:(2 * i + 2) * 512]
            act(XS[:], yv, AF.Square, [("ps", 2 * i), ("ps", 2 * i + 1)], ["XS", ("SS", i)], accum_out=SS[:, i:i + 1])
            act(LNV[:, i:i + 1], SS[:, i:i + 1], AF.Ln, [("SS", i), "EPSC"], [("LNV", i)], scale=1.0 / D, bias=EPSC[:, 0:1])
            act(RSTD[:, i:i + 1], LNV[:, i:i + 1], AF.Exp, [("LNV", i)], [("RSTD", i)], scale=-0.5)
            for hh in range(2):
                ti = (i % 2) * 2 + hh
                tmp = T[ti][:, 0:512]
                stt(tmp, PS(2 * i + hh), RSTD[:, i:i + 1], GG[which][:, hh * 512:(hh + 1) * 512], ALU.mult, ALU.mult,
                    [("ps", 2 * i + hh), ("RSTD", i), ("GG", which)], [f"T{ti}"])
                xa = X[:, tile0 + i, hh * 512:(hh + 1) * 512]
                tt(xa, xa, tmp, ALU.add, [f"T{ti}", ("X", tile0 + i)], [("X", tile0 + i)], eng="pool")

    def pass2(l, gi):
        t0, NT, s = GROUPS[gi]
        ntt = NT // 128
        tile0 = t0 // 128
        norm_to_T(gi, 0, "cached")
        load_tabs(gi)
        fence(["Ra", "Rb"])
        S.phase = "qproj"
        wa, wk = wload(wp2_d[l][:, 0:512], (8, 512))
        for j in range(2):
            b1, b2 = nb(), nb()
            proj(b1, wa[:, :, j * 128:(j + 1) * 128], wk, NT)
            proj(b2, wa[:, :, 256 + j * 128:256 + (j + 1) * 128], wk, NT)
            normrope(QTA[:, j, 0:NT], [("QTA", j)], b1, b2, NT, l, 80, extra_r=["Ra"])
        wb, wkb = wload(wp2_d[l][:, 512:1024], (8, 512))
        wc, wkc = wload(wp2_d[l][:, 1024:1536], (8, 512))
        L = NT
        RS = WBS[0][:].rearrange("p a b -> p (a b)").bitcast(F32)
        rsk = ("wbs", 0)
        QBN = UE[:].rearrange("p a b -> p (a b)")[:, 0:1024].rearrange("p (a n) -> p a n", a=2)

        def filler():
            sq2 = XS[:].rearrange("p (a n) -> p a n", a=2)
            for c in range(2):
                proj(6 + c, wc[:, :, c * 128:(c + 1) * 128], wkc, NT)
                act(sq2[:, c, 0:NT], PS(6 + c)[:, 0:NT], AF.Square, [("ps", 6 + c)], ["XS"])
                cp(QBN[:, c, 0:NT], PS(6 + c)[:, 0:NT], [("ps", 6 + c)], ["UE"])
                yield "x"
            for c in range(2):
                mm(PS(6)[:, 0:NT], ones, sq2[:, c, 0:NT], c == 0, c == 1, ["XS", "const"], [("ps", 6)])
            rsqrt_big(RS[:, 0:NT], PS(6)[:, 0:NT], NT, 1.0 / 256, [("ps", 6)], [rsk])
            for c in range(2):
                stt(QBN[:, c, 0:NT], QBN[:, c, 0:NT], VEC[:, l, 84 + c:85 + c], RS[:, 0:NT], ALU.mult, ALU.mult,
                    ["UE", "VEC", rsk], ["UE"])
            yield "x"
            for h in range(4):
                for kc in range(2):
                    mm(PS(6)[0:96, 0:NT], SMW[:, kc * 384 + h * 96:kc * 384 + (h + 1) * 96], QBN[:, kc, 0:NT],
                       kc == 0, kc == 1, ["SMW", "UE"], [("ps", 6)])
                for kc in range(2):
                    mm(PS(7)[0:96, 0:NT], SMW[:, 768 + kc * 384 + h * 96:768 + kc * 384 + (h + 1) * 96],
                       QBN[:, kc, 0:NT], kc == 0, kc == 1, ["SMW", "UE"], [("ps", 7)])
                a, b = rope_plain(None, None, 6, 7, NT, rows=96, cosi=2)
                tt(QTB[0:96, h, 0:NT], a, b, ALU.add, ["T3", "Ra"], [("QTB", h)])
                yield "x"
            yield "qb_done"
            for j in range(2):
                proj(6, wb[:, :, j * 128:(j + 1) * 128], wkb, NT)
                yield "x"
                proj(7, wb[:, :, 256 + j * 128:256 + (j + 1) * 128], wkb, NT)
                a, b = rope_plain(None, None, 6, 7, NT)
                tt(QTC[:, j, 0:NT], a, b, ALU.add, ["T3", "Ra"], [("QTC", j)])
                yield "x"
            yield "qc_done"
            S.phase = "pool"
            for c in range(2):
                proj(6, wc[:, :, 256 + c * 128:256 + (c + 1) * 128], wkc, NT)
                if c == 0:
                    pass
                act(UE[:, c, 8:8 + L], PS(6)[:, 0:NT], AF.Copy, [("ps", 6)], ["UE"])
                if gi >= 2:
                    cp(UE[:, c, 0:8], HALO[:, c, gi - 1, 8:16], [("HALO", gi - 1)], ["UE"])
                else:
                    memset(UE[:, c, 0:8], 0.0, ["UE"])
                if 1 <= gi <= 3:
                    cp(UE[:, c, 8 + L:16 + L], HALO[:, c, gi + 1, 0:8], [("HALO", gi + 1)], ["UE"])
                else:
                    memset(UE[:, c, 8 + L:16 + L], 0.0, ["UE"])
                yield "x"
            tA = Tb(3)[:, 0:544]
            tB = Tb(3)[:, 544:1088]
            for c in range(2):
                for hb in range(2):
                    w_ = POOLW[2 * c + hb]
                    pr = slice(hb * 64, (hb + 1) * 64)
                    u = UE[pr, c, :]
                    if w_ == 2:
                        tt(tB[pr, 0:L], u[:, 7:7 + L], u[:, 8:8 + L], ALU.add, ["UE"], ["T3"])
                    elif w_ == 4:
                        tt(tA[pr, 0:L + 2], u[:, 6:8 + L], u[:, 7:9 + L], ALU.add, ["UE"], ["T3"])
                        tt(tB[pr, 0:L], tA[pr, 0:L], tA[pr, 2:L + 2], ALU.add, ["T3"], ["T3"])
                    elif w_ == 8:
                        tt(tA[pr, 0:L + 6], u[:, 4:10 + L], u[:, 5:11 + L], ALU.add, ["UE"], ["T3"])
                        tt(tB[pr, 0:L + 4], tA[pr, 0:L + 4], tA[pr, 2:L + 6], ALU.add, ["T3"], ["T3"])
                        tt(tA[pr, 0:L], tB[pr, 0:L], tB[pr, 4:L + 4], ALU.add, ["T3"], ["T3"])
                        cp(tB[pr, 0:L], tA[pr, 0:L], ["T3"], ["T3"])
                    else:
                        tt(tA[pr, 0:L + 14], u[:, 0:14 + L], u[:, 1:15 + L], ALU.add, ["UE"], ["T3"])
                        tt(tB[pr, 0:L + 12], tA[pr, 0:L + 12], tA[pr, 2:L + 14], ALU.add, ["T3"], ["T3"])
                        tt(tA[pr, 0:L + 8], tB[pr, 0:L + 8], tB[pr, 4:L + 12], ALU.add, ["T3"], ["T3"])
                        tt(tB[pr, 0:L], tA[pr, 0:L], tA[pr, 8:L + 8], ALU.add, ["T3"], ["T3"])
                    yield "x"
                tt(tA[:, 0:L], tB[:, 0:L], TABg[:, 4 + c, 0:L], ALU.mult, ["T3", "TAB"], ["T3"])
                pm = XS[:, 0:L]
                tt(pm, tA[:, 0:L], UE[:, c, 8:8 + L], ALU.subtract, ["T3", "UE"], ["XS"])
                mm(PS(7)[:, 0:L], SMW[:, 2048 + c * 128:2048 + (c + 1) * 128], pm, True, True, ["XS", "SMW"], [("ps", 7)])
                act(OTg[:, 6 + c, 0:L], PS(7)[:, 0:L], AF.Identity, [("ps", 7), "VEC", "Rb"], [("OT", 6 + c)],
                    scale=VEC[:, l, 89 + c:90 + c])
                yield "x"
            S.phase = "attn"
            yield "pool_done"

        S.phase = "attn"
        if s == 1:
            keys_full = [(kt, 0, NT, []) for kt in range(2)]
        else:
            keys_full = [(kt, 0, NT, []) for kt in range(18)]
        if s == 1:
            keys_c = [(kt, 0, NT, []) for kt in range(2)]
        else:
            qb0 = 4 * (gi - 1)
            keys_c = [(kt, 0, NT, []) for kt in range(2)]
            for kb in range(max(0, qb0 - 1), min(15, qb0 + 4) + 1):
                ilo = max(0, kb - 1 - qb0)
                ihi = min(3, kb + 1 - qb0)
                masks = []
                for i in range(ilo, ihi + 1):
                    qblk = qb0 + i
                    if kb == qblk - 1:
                        masks.append((i * 128, (i + 1) * 128, maskL))
                    elif kb == qblk + 1:
                        masks.append((i * 128, (i + 1) * 128, maskU))
                keys_c.append((2 + kb, ilo * 128, (ihi + 1) * 128, masks))

        def att_a(j):
            return attention(NT, QTA[0:64, j, :], QTA[64:128, j, :],
                             lambda kt: KTA[0:64, kt * 128:(kt + 1) * 128], lambda kt: KTA[64:128, kt * 128:(kt + 1) * 128],
                             lambda kt: VAC[:, kt, 0:64], lambda kt: VAC[:, kt, 64:128],
                             64, keys_full, 0.125, j, None, [("QTA", j)], "KTA", "VAC")

        def att_b(j):
            ha, hb_ = 2 * j, 2 * j + 1
            return attention(NT, QTB[0:96, ha, :], QTB[0:96, hb_, :],
                             lambda kt, h=ha: KTB[0:96, h, kt * 128:(kt + 1) * 128],
                             lambda kt, h=hb_: KTB[0:96, h, kt * 128:(kt + 1) * 128],
                             lambda kt, h=ha: VB[:, kt, h * 64:(h + 1) * 64],
                             lambda kt, h=hb_: VB[:, kt, h * 64:(h + 1) * 64],
                             96, keys_full, 96 ** -0.5, 2 + j, None, [("QTB", ha), ("QTB", hb_)], "KTB", "VB")

        def att_c(j):
            return attention(NT, QTC[0:64, j, :], QTC[64:128, j, :],
                             lambda kt: KTC[0:64, kt * 128:(kt + 1) * 128], lambda kt: KTC[64:128, kt * 128:(kt + 1) * 128],
                             lambda kt: VAC[:, kt, 128:192], lambda kt: VAC[:, kt, 192:256],
                             64, keys_c, 0.125, 4 + j, j, [("QTC", j)], "KTC", "VAC")

        fg = filler()
        fstate = {"qb_done": False, "qc_done": False, "pool_done": False}

        def fstep():
            try:
                r = next(fg)
                if r in fstate:
                    fstate[r] = True
                return True
            except StopIteration:
                return False

        def drain(flag):
            while not fstate[flag]:
                if not fstep():
                    break

        for kind, need in (("a", None), ("b", "qb_done"), ("c", "qc_done")):
            if need is not None:
                drain(need)
            for j in range(2):
                g = {"a": att_a, "b": att_b, "c": att_c}[kind](j)
                for _ in g:
                    fstep()
        drain("pool_done")
        S.phase = "gate"
        fence(["Ra"])
        otk = [("OT", i) for i in range(8)]
        for dc in range(8):
            wg, wk = wload(wp2_d[l][:, 1536 + dc * 512:1536 + (dc + 1) * 512], (8, 512))
            wbp, wbk = wbload(wbp_d[l][dc])
            for k in range(4):
                gb = k % 2
                pb = 2 + k % 2
                for kc in range(8):
                    mm(PS(gb)[:, 0:NT], wg[:, kc, k * 128:(k + 1) * 128], HT[:, kc, 0:NT], kc == 0, kc == 7,
                       [wk, "HT"], [("ps", gb)])
                for kc2 in range(2):
                    mm(PS(pb)[:, 0:NT], wbp[:, 2 * k + kc2, :], OTg[:, 2 * k + kc2, 0:NT], kc2 == 0, kc2 == 1,
                       [wbk, ("OT", 2 * k + kc2), "Rb"], [("ps", pb)])
                sg = T[k % 2][:, 0:NT]
                act(sg, PS(gb)[:, 0:NT], AF.Sigmoid, [("ps", gb)], [f"T{k % 2}"])
                if k == 0:
                    tt(T[2][:, 0:NT], PS(pb)[:, 0:NT], sg, ALU.mult, [("ps", pb), f"T{k % 2}"], ["T2"])
                else:
                    tt(T[3][:, 0:NT], PS(pb)[:, 0:NT], sg, ALU.mult, [("ps", pb), f"T{k % 2}"], ["T3"])
                    if k < 3:
                        tt(T[2][:, 0:NT], T[2][:, 0:NT], T[3][:, 0:NT], ALU.add, ["T2", "T3"], ["T2"])
                    else:
                        tt(sTg[:, dc, 0:NT], T[2][:, 0:NT], T[3][:, 0:NT], ALU.add, ["T2", "T3", "Ra"], [("sT", dc)])
        S.phase = "wout"
        for half in range(2):
            wo, wk = wload(wo_d[l][half * 512:(half + 1) * 512, :], (4, 1024))
            for kcl in range(4):
                kc = half * 4 + kcl
                for i in range(ntt):
                    for hh in range(2):
                        mm(PS(2 * i + hh), sTg[:, kc, i * 128:(i + 1) * 128], wo[:, kcl, hh * 512:(hh + 1) * 512],
                           kc == 0, kc == 7, [wk, ("sT", kc), "Ra"], [("ps", 2 * i + hh)])
        S.phase = "postnorm"
        post_norm(gi, 0)
        norm_to_T(gi, 1)
        fence(["Ra", "Rb"])
        S.phase = "ffn13"
        for jp in range(11):
            w, wk = wload(w13_d[l][:, jp * 512:(jp + 1) * 512], (8, 512))
            for jj in range(2):
                j = 2 * jp + jj
                b1 = 2 * (j % 2)
                b3 = b1 + 1
                for kc in range(8):
                    mm(PS(b1)[:, 0:NT], w[:, kc, jj * 256:jj * 256 + 128], HT[:, kc, 0:NT], kc == 0, kc == 7,
                       [wk, "HT"], [("ps", b1)])
                for kc in range(8):
                    mm(PS(b3)[:, 0:NT], w[:, kc, jj * 256 + 128:jj * 256 + 256], HT[:, kc, 0:NT], kc == 0, kc == 7,
                       [wk, "HT"], [("ps", b3)])
                sl = T[j % 2][:, 0:NT]
                act(sl, PS(b1)[:, 0:NT], AF.Silu, [("ps", b1)], [f"T{j % 2}"])
                tt(gT[:, j, 0:NT], PS(b3)[:, 0:NT], sl, ALU.mult, [("ps", b3), f"T{j % 2}", "Ra", "Rb"], [("gT", j)])
        S.phase = "ffn2"
        for pc in range(6):
            nj = 4 if pc < 5 else 2
            w2, wk = wload(w2_d[l][pc * 512:pc * 512 + nj * 128, :], (nj, 1024))
            for jl in range(nj):
                j = pc * 4 + jl
                for i in range(ntt):
                    for hh in range(2):
                        mm(PS(2 * i + hh), gT[:, j, i * 128:(i + 1) * 128], w2[:, jl, hh * 512:(hh + 1) * 512],
                           j == 0, j == NJ - 1, [wk, ("gT", j), "Ra", "Rb"], [("ps", 2 * i + hh)])
        S.phase = "postnorm"
        post_norm(gi, 1)

    def forward():
        if dbg == 1:
            return
        for l in range(depth):
            with_ctx = l < depth - 1
            p0(l)
            if dbg == 2:
                return
            for gi in range(5):
                pass1(l, gi)
                if dbg == 3:
                    return
            if dbg == 4:
                return
            if with_ctx:
                gen_gg(0, 1)
                gen_gg(1, 1)
                pass2(l, 0)
            gen_gg(0, 0)
            gen_gg(1, 0)
            if dbg == 5:
                return
            for gi in range(1, 5):
                pass2(l, gi)
                if dbg == 6:
                    return

    try:
        forward()
    except StopBuild:
        pass
    for i in range(4):
        dma_sp(out_d[512 * i:512 * (i + 1), :].rearrange("(t p) d -> p t d", p=128), X[:, 2 + 4 * i:6 + 4 * i, :],
               [("X", 2 + 4 * i + k) for k in range(4)], [])
    if not emit:
        return S
    S.emit()
    return nc


def _swap_heads(w, hd):
    k, n = w.shape
    w4 = w.reshape(k, n // hd, 2, hd // 2)
    return np.ascontiguousarray(w4[:, :, ::-1, :]).reshape(k, n)


def _consts():
    bf = ml_dtypes.bfloat16
    cb = np.zeros((128, 640), np.float32)
    cb[:, 0:128] = np.eye(128)
    cb[:, 128:256] = 1.0
    cb[0:64, 256:320] = 1.0
    cb[64:128, 320:384] = 1.0
    j = np.arange(128)[:, None]
    i = np.arange(128)[None, :]
    cb[:, 384:512] = (j >= i)
    cb[:, 512:640] = (j <= i)
    t = np.arange(SEQ)
    row = (t // 64).astype(np.float32)
    col = (t % 64).astype(np.float32)

    def angles(rot_dim):
        nf = rot_dim // 4
        inv = (np.float32(10000.0) ** (-np.arange(nf, dtype=np.float32) / nf)).astype(np.float32)
        ang = np.concatenate([row[:, None] * inv, col[:, None] * inv], axis=-1).astype(np.float32)
        return np.cos(ang).astype(np.float32), np.sin(ang).astype(np.float32)

    c64, s64 = angles(64)
    c32, s32 = angles(32)
    tab = np.zeros((6, 128, NTOK), np.float32)
    tab[0, :, :CTX] = 1.0
    tab[2, :, :] = 1.0
    for p in range(128):
        d = p % 64
        f = d % 32
        tab[0, p, CTX:] = c64[:, f]
        tab[1, p, CTX:] = (-s64[:, f]) if d < 32 else s64[:, f]
    for p in range(64, 96):
        r = p - 64
        f = r % 16
        tab[2, p, CTX:] = c32[:, f]
        tab[3, p, CTX:] = (-s32[:, f]) if r < 16 else s32[:, f]
    for c in range(2):
        for hb in range(2):
            w = POOLW[2 * c + hb]
            lo = w // 2
            hi = w - lo - 1
            for (o0, Tn) in ((0, CTX), (CTX, SEQ)):
                pos = np.arange(Tn)
                st = np.clip(pos - lo, 0, Tn)
                en = np.clip(pos + hi + 1, 0, Tn)
                tab[4 + c, hb * 64:(hb + 1) * 64, o0:o0 + Tn] = (1.0 / (en - st).astype(np.float32))[None, :]
    return cb.astype(bf), tab.astype(bf)


def prep_shared(inp):
    f = lambda k: np.asarray(inp[k], np.float32)
    w_in = f("w_in")
    L = w_in.shape[0]
    o = np.cumsum([0, 256, 128, 128, 256, 128, 32, 256, 128, 128, 256, 4096])
    qa, ka, va, qba, kvba, kbr, qc, kc, vc, u, gl = [w_in[:, :, o[i]:o[i + 1]] for i in range(11)]
    hperm = np.concatenate([np.arange(0, 64), np.arange(128, 192), np.arange(64, 128), np.arange(192, 256)])
    wp1 = np.zeros((L, D, 1344), np.float32)
    wp2 = np.zeros((L, D, 5632), np.float32)
    wbp = np.zeros((L, 8, D, 128), np.float32)
    smw = np.zeros((L, 128, 2304), np.float32)
    vec = np.zeros((128, L, NV), np.float32)
    w13 = np.zeros((L, D, NJ * 256), np.float32)
    w_b = f("w_branch")
    for l in range(L):
        wp1[l, :, 0:128] = ka[l]
        wp1[l, :, 128:256] = _swap_heads(ka[l], 64)
        wp1[l, :, 256:384] = kc[l]
        wp1[l, :, 384:512] = _swap_heads(kc[l], 64)
        wp1[l, :, 512:640] = kvba[l]
        wp1[l, :, 640 + 64:640 + 96] = kbr[l]
        wp1[l, :, 736 + 64:736 + 96] = _swap_heads(kbr[l], 32)
        wp1[l, :, 832:960] = va[l]
        wp1[l, :, 960:1088] = vc[l]
        wp1[l, :, 1088:1344] = u[l]
        qap = qa[l][:, hperm]
        wp2[l, :, 0:256] = qap
        wp2[l, :, 256:512] = _swap_heads(qap, 64)
        qcp = qc[l][:, hperm]
        wp2[l, :, 512:768] = qcp
        wp2[l, :, 768:1024] = _swap_heads(qcp, 64)
        wp2[l, :, 1024:1280] = qba[l]
        wp2[l, :, 1280:1536] = u[l]
        g4 = gl[l].reshape(D, 4, 8, 128)
        wp2[l, :, 1536:] = np.ascontiguousarray(g4.transpose(0, 2, 1, 3)).reshape(D, 4096)
        for k in range(4):
            wk_ = w_b[l, k]
            if k in (0, 2):
                wk_ = wk_[hperm, :]
            for dc in range(8):
                wbp[l, dc, k * 256:(k + 1) * 256, :] = wk_[:, dc * 128:(dc + 1) * 128]
        wqb = f("b_w_qb")[l]
        wqbs = np.zeros_like(wqb)
        for h in range(4):
            wqbs[:, h * 96 + 64:h * 96 + 96] = _swap_heads(wqb[:, h * 96 + 64:h * 96 + 96], 32)
        smw[l, :, 0:768] = wqb.reshape(2, 128, 384).transpose(1, 0, 2).reshape(128, 768)
        smw[l, :, 768:1536] = wqbs.reshape(2, 128, 384).transpose(1, 0, 2).reshape(128, 768)
        wkvb = f("b_w_kvb")[l].reshape(128, 4, 128)
        smw[l, :, 1536:1792] = wkvb[:, :, 0:64].reshape(128, 256)
        smw[l, :, 1792:2048] = wkvb[:, :, 64:128].reshape(128, 256)
        wpool = f("d_w_pool")[l]
        for c in range(2):
            for hb in range(2):
                smw[l, hb * 64:(hb + 1) * 64, 2048 + c * 128 + hb * 64:2048 + c * 128 + (hb + 1) * 64] = wpool[2 * c + hb]
        vec[:, l, 0:48] = f("ada_b")[l].reshape(48, 128).T
        vec[:, l, 48:56] = f("mix_pre_g")[l].reshape(8, 128).T
        vec[:, l, 56:64] = f("ffn_pre_g")[l].reshape(8, 128).T
        vec[:, l, 64:72] = f("mix_post_g")[l].reshape(8, 128).T
        vec[:, l, 72:80] = f("ffn_post_g")[l].reshape(8, 128).T
        for cbase, g in ((80, f("a_qn_g")[l]), (82, f("a_kn_g")[l])):
            vec[:, l, cbase] = np.tile(g, 2)
            vec[:, l, cbase + 1] = np.tile(np.concatenate([g[32:], g[:32]]), 2)
        vec[:, l, 84:86] = f("b_qa_g")[l].reshape(2, 128).T
        vec[:, l, 86] = f("b_kva_g")[l]
        sk = f("c_sink")[l]
        vec[0:64, l, 87] = sk[0]
        vec[64:128, l, 87] = sk[2]
        vec[0:64, l, 88] = sk[1]
        vec[64:128, l, 88] = sk[3]
        vec[:, l, 89:91] = f("d_scale")[l].reshape(2, 128).T
        w1 = f("w_ffn1")[l].reshape(D, NJ, 128)
        w3 = f("w_ffn3")[l].reshape(D, NJ, 128)
        w13[l] = np.stack([w1, w3], axis=2).reshape(D, NJ * 256)
    cb, tab = _consts()
    return {
        "vec": np.ascontiguousarray(vec.reshape(128, L * NV)),
        "constb": cb, "tab": tab,
        "ada": f("ada_w"), "wp1": wp1, "wp2": wp2, "wbp": wbp, "wo": f("w_out"),
        "w13": w13, "w2": f("w_ffn2"), "smw": smw,
    }


def prep_core(inp, b):
    c = np.asarray(inp["c"], np.float32)[b]
    cc = np.asarray(inp["c_ctx"], np.float32)
    cT = np.stack([c.reshape(8, 128).T, cc.reshape(8, 128).T], axis=2).reshape(128, 16)
    return {
        "x": np.ascontiguousarray(np.asarray(inp["x"], np.float32)[b]),
        "ctx": np.ascontiguousarray(np.asarray(inp["ctx"], np.float32)[b]),
        "cT": np.ascontiguousarray(cT),
    }


def kernel(**inputs):
    nb_ = inputs["x"].shape[0]
    shared = prep_shared(inputs)
    nc = build_program(depth=4)
    in_maps = []
    for b in range(nb_):
        m = dict(shared)
        m.update(prep_core(inputs, b))
        in_maps.append(m)
    res = run_bass_kernel_spmd(nc, in_maps, core_ids=list(range(nb_)))
    return np.stack([np.asarray(r["out"], np.float32) for r in res.results], axis=0)
```

```python
import numpy as np
import ml_dtypes
import concourse.bass as bass
import concourse.mybir as mybir
from concourse.bass_utils import run_bass_kernel_spmd

F32 = mybir.dt.float32
BF16 = mybir.dt.bfloat16
ALU = mybir.AluOpType
AF = mybir.ActivationFunctionType

ENGS = ("pe", "act", "dve", "pool", "sp")
NDS = 8

D = 1024
NTOK = 2304
CTX = 256
SEQ = 2048
DFF = 2816
NJ = 22
EPS = 1e-6
GROUPS = [(0, 256, 1)] + [(256 + 512 * i, 512, 0) for i in range(4)]
POOLW = (2, 4, 8, 16)
NV = 96
TW = 544


class Op:
    __slots__ = ("eng", "fn", "dma", "deps", "signal", "semval", "idx", "dsem", "dval", "phase")

    def __init__(self, eng, fn, dma):
        self.eng = eng
        self.fn = fn
        self.dma = dma
        self.deps = []
        self.signal = False
        self.semval = 0


class Sched:
    def __init__(self, nc):
        self.nc = nc
        self.ops = {e: [] for e in ENGS}
        self.last_w = {}
        self.readers = {}
        self.ndma = {e: 0 for e in ENGS}
        self.phase = ""

    def op(self, eng, fn, reads=(), writes=(), dma=False):
        o = Op(eng, fn, dma)
        o.phase = self.phase
        deps = {}

        def add(d, kind):
            if d is None or d is o:
                return
            if not d.dma and not o.dma and d.eng == eng:
                if eng == "pe":
                    return
                if kind != "raw":
                    return
            deps[id(d)] = d

        for k in reads:
            add(self.last_w.get(k), "raw")
            if isinstance(k, tuple) and k[0] == "ps" and eng != "pe":
                for r in self.readers.get(k, ()):
                    if r.eng != eng:
                        add(r, "war")
        for k in writes:
            add(self.last_w.get(k), "waw")
            for r in self.readers.get(k, ()):
                add(r, "war")
        for k in reads:
            lst = self.readers.setdefault(k, [])
            if not dma:
                lst[:] = [r for r in lst if r.dma or r.eng != eng]
            lst.append(o)
        for k in writes:
            self.last_w[k] = o
            self.readers[k] = []
        o.deps = list(deps.values())
        if dma:
            o.idx = self.ndma[eng]
            self.ndma[eng] += 1
        self.ops[eng].append(o)
        return o

    def emit(self):
        nc = self.nc
        used = [e for e in ENGS if self.ops[e]]
        for e in used:
            for o in self.ops[e]:
                for d in o.deps:
                    d.signal = True
        csem = {e: nc.alloc_semaphore(f"c_{e}") for e in used}
        dsem = {e: [nc.alloc_semaphore(f"d_{e}_{i}") for i in range(NDS)] for e in used if self.ndma[e]}
        for e in used:
            cnt = 0
            for o in self.ops[e]:
                if o.dma:
                    o.dsem = dsem[e][o.idx % NDS]
                    o.dval = 16 * (o.idx // NDS + 1)
                elif o.signal:
                    cnt += 1
                    o.semval = cnt
        with nc.Block() as block:
            for e in used:
                deco = {"pe": block.tensor, "act": block.scalar, "dve": block.vector,
                        "pool": block.gpsimd, "sp": block.sync}[e]
                ops = self.ops[e]

                def body(engine, ops=ops, e=e):
                    waited = {}

                    def wait(sem, val):
                        key = id(sem)
                        if waited.get(key, 0) >= val:
                            return
                        waited[key] = val
                        engine.wait_ge(sem, val)

                    dmas = []
                    for o in ops:
                        for d in o.deps:
                            if d.dma:
                                wait(d.dsem, d.dval)
                            else:
                                wait(csem[d.eng], d.semval)
                        if o.dma:
                            if o.idx >= NDS:
                                wait(o.dsem, o.dval - 16)
                            o.fn().then_inc(o.dsem, 16)
                            dmas.append(o)
                        else:
                            ins = o.fn()
                            if o.signal:
                                ins.then_inc(csem[e], 1)
                    for o in dmas[-NDS:]:
                        wait(o.dsem, o.dval)

                deco(body)


class StopBuild(Exception):
    pass


def build_program(depth=4, dbg=False, emit=True):
    nc = bass.Bass("TRN2", target_bir_lowering=False)
    S = Sched(nc)

    def dram(name, shape, dt=F32, out=False):
        return nc.dram_tensor(name, list(shape), dt, kind="ExternalOutput" if out else "ExternalInput").ap()

    x_d = dram("x", [SEQ, D])
    ctx_d = dram("ctx", [CTX, D])
    cT_d = dram("cT", [128, 16])
    vec_d = dram("vec", [128, 4 * NV])
    constb_d = dram("constb", [128, 640], BF16)
    tab_d = dram("tab", [6, 128, NTOK], BF16)
    ada_d = dram("ada", [4, D, 6 * D])
    wp1_d = dram("wp1", [4, D, 1344])
    wp2_d = dram("wp2", [4, D, 5632])
    wbp_d = dram("wbp", [4, 8, D, 128])
    wo_d = dram("wo", [4, D, D])
    w13_d = dram("w13", [4, D, NJ * 256])
    w2_d = dram("w2", [4, DFF, D])
    smw_d = dram("smw", [4, 128, 2304])
    out_d = dram("out", [SEQ, D], out=True)

    sb = nc.alloc_sbuf_tensor
    X = sb("X", [128, 18, D], F32)
    KTA = sb("KTA", [128, NTOK], BF16)
    KTC = sb("KTC", [128, NTOK], BF16)
    VAC = sb("VAC", [128, 18, 256], BF16)
    KTB = sb("KTB", [96, 4, NTOK], BF16)
    VB = sb("VB", [128, 18, 256], BF16)
    HT = sb("HT", [128, 8, 512], BF16)
    R1 = sb("R1", [128, NJ * 512], BF16)
    GG = [sb(f"GG{i}", [128, D], BF16) for i in range(2)]
    TABg = sb("TABg", [128, 6, 512], BF16)
    WS = [sb(f"WS{i}", [128, 4096], BF16) for i in range(3)]
    WBS = [sb(f"WBS{i}", [128, 8, 128], BF16) for i in range(2)]
    SMW = sb("SMW", [128, 2304], BF16)
    VEC = sb("VEC", [128, 4, NV], F32)
    CONSTB = sb("CONSTB", [128, 640], BF16)
    MODT = sb("MODT", [128, 48, 2], F32)
    PRM = sb("PRM", [128, 4, 8, 2], F32)
    ES = sb("ES", [128, 2], F32)
    cT = sb("cTs", [128, 16], F32)
    scb = sb("scb", [128, 8, 2], BF16)
    XS = sb("XS", [128, D], BF16)
    T = [sb(f"T{i}", [128, TW], F32) for i in range(4)]
    UE = sb("UE", [128, 2, 528], BF16)
    HALO = sb("HALO", [128, 2, 5, 16], BF16)
    DIAG = [sb(f"DIAG{i}", [128, 128], BF16) for i in range(2)]
    SS = sb("SS", [128, 4], F32)
    LNV = sb("LNV", [128, 4], F32)
    RSTD = sb("RSTD", [128, 4], F32)
    RSTDALL = sb("RSTDALL", [128, 18], F32)
    EPSC = sb("EPSC", [128, 1], F32)
    FSC = sb("FSC", [128, 2], F32)
    PSALL = nc.alloc_psum_tensor("PSALL", [128, 4096], F32)

    ident = CONSTB[:, 0:128]
    ones = CONSTB[:, 128:256]
    bd = CONSTB[:, 256:384]
    maskL = CONSTB[:, 384:512]
    maskU = CONSTB[:, 512:640]

    def chk(n):
        if dbg == n:
            raise StopBuild()

    def PS(b):
        return PSALL[:, b * 512:(b + 1) * 512]

    TP2 = PSALL[:, 0:2048].bitcast(BF16)
    TP4 = TP2.rearrange("p (c i x) -> p c i x", c=8, i=4)

    QTA = R1[:, 0:1024].rearrange("p (a n) -> p a n", a=2)
    QTC = R1[:, 1024:2048].rearrange("p (a n) -> p a n", a=2)
    QTB = R1[:, 2048:4096].rearrange("p (a n) -> p a n", a=4)
    PT = [R1[:, 4096 + i * 512: 4096 + (i + 1) * 512] for i in range(4)]
    OTg = R1[:, 6144:10240].rearrange("p (a n) -> p a n", a=8)
    sTg = R1[:, 0:4096].rearrange("p (a n) -> p a n", a=8)
    gT = R1[:, 0:NJ * 512].rearrange("p (a n) -> p a n", a=NJ)

    def Tb(i):
        return T[i][:].bitcast(BF16)

    def mm(out, lhsT, rhs, start, stop, r, w):
        S.op("pe", lambda: nc.tensor.matmul(out, lhsT=lhsT, rhs=rhs, start=start, stop=stop), reads=r, writes=w)

    def tr(out, in_, r, w):
        S.op("pe", lambda: nc.tensor.transpose(out, in_, ident), reads=list(r) + ["const"], writes=w)

    def act(out, in_, func, r, w, scale=None, bias=None, accum_out=None):
        kw = {}
        if scale is not None:
            kw["scale"] = scale
        if bias is not None:
            kw["bias"] = bias
        if accum_out is not None:
            kw["accum_out"] = accum_out
        S.op("act", lambda: nc.scalar.activation(out=out, in_=in_, func=func, **kw), reads=r, writes=w)

    def tt(out, in0, in1, op, r, w, eng="dve"):
        if eng == "pool":
            S.op("pool", lambda: nc.gpsimd.tensor_tensor(out=out, in0=in0, in1=in1, op=op), reads=r, writes=w)
        else:
            S.op("dve", lambda: nc.vector.tensor_tensor(out=out, in0=in0, in1=in1, op=op), reads=r, writes=w)

    def ts(out, in0, s1, s2, op0, op1, r, w):
        if op1 is None:
            S.op("dve", lambda: nc.vector.tensor_scalar(out=out, in0=in0, scalar1=s1, scalar2=None, op0=op0),
                 reads=r, writes=w)
        else:
            S.op("dve", lambda: nc.vector.tensor_scalar(out=out, in0=in0, scalar1=s1, scalar2=s2, op0=op0, op1=op1),
                 reads=r, writes=w)

    def stt(out, in0, scalar, in1, op0, op1, r, w):
        S.op("dve", lambda: nc.vector.scalar_tensor_tensor(out=out, in0=in0, scalar=scalar, in1=in1, op0=op0, op1=op1),
             reads=r, writes=w)

    def cp(out, in_, r, w):
        S.op("dve", lambda: nc.vector.tensor_copy(out=out, in_=in_), reads=r, writes=w)

    def recip(out, in_, r, w):
        S.op("dve", lambda: nc.vector.reciprocal(out=out, in_=in_), reads=r, writes=w)

    def memset(ap, val, w):
        S.op("dve", lambda: nc.vector.memset(ap, val), writes=w)

    def dma_sp(out, in_, r, w):
        S.op("sp", lambda: nc.sync.dma_start(out=out, in_=in_), reads=r, writes=w, dma=True)

    def fence(keys):
        S.op("dve", lambda: nc.vector.memset(FSC[:, 0:1], 0.0), writes=list(keys) + ["fsc"])

    wctr = [0]

    def wload(src_ap, shape3):
        i = wctr[0] % 3
        wctr[0] += 1
        k, n = shape3
        dst = WS[i][:, 0:k * n].rearrange("p (k n) -> p k n", k=k)
        src = src_ap.rearrange("(k p) n -> p k n", p=128)
        key = ("ws", i)
        S.op("pool", lambda: nc.gpsimd.dma_start(out=dst, in_=src), writes=[key], dma=True)
        return dst, key

    wbctr = [0]

    def wbload(src_ap):
        i = wbctr[0] % 2
        wbctr[0] += 1
        dst = WBS[i][:]
        src = src_ap.rearrange("(k p) n -> p k n", p=128)
        key = ("wbs", i)
        S.op("pool", lambda: nc.gpsimd.dma_start(out=dst, in_=src), writes=[key], dma=True)
        return dst, key

    bankctr = [0]

    def nb(lo=0, n=8):
        b = lo + bankctr[0] % n
        bankctr[0] += 1
        return b

    def gkey(name, kt):
        return (name, 0 if kt < 2 else 1 + (kt - 2) // 4)

    dma_sp(CONSTB[:], constb_d, [], ["const"])
    dma_sp(VEC[:].rearrange("p l v -> p (l v)"), vec_d, [], ["VEC"])
    dma_sp(cT[:], cT_d, [], ["cT"])
    dma_sp(X[:, 0:2, :], ctx_d.rearrange("(t p) d -> p t d", p=128), [], [("X", 0), ("X", 1)])
    for i in range(4):
        dma_sp(X[:, 2 + 4 * i: 6 + 4 * i, :], x_d[512 * i:512 * (i + 1), :].rearrange("(t p) d -> p t d", p=128),
               [], [("X", 2 + 4 * i + k) for k in range(4)])
    memset(EPSC[:], EPS, ["EPSC"])
    act(scb[:].rearrange("p k s -> p (k s)"), cT[:], AF.Silu, ["cT"], ["scb"])

    def rsqrt_small(n, scale):
        act(LNV[:, 0:n], SS[:, 0:n], AF.Ln, ["SS", "EPSC"], ["LNV"], scale=scale, bias=EPSC[:, 0:1])
        act(RSTD[:, 0:n], LNV[:, 0:n], AF.Exp, ["LNV"], ["RSTD"], scale=-0.5)

    def rsqrt_big(out, in_, n, scale, r, w):
        act(out, in_, AF.Ln, list(r) + ["EPSC"], w, scale=scale, bias=EPSC[:, 0:1])
        act(out, out, AF.Exp, w, w, scale=-0.5)

    def p0(l):
        S.phase = "p0"
        modps = PS(6)[:, 0:96].rearrange("p (c s) -> p c s", s=2)
        for pc in range(12):
            w, wk = wload(ada_d[l][:, pc * 512:(pc + 1) * 512], (8, 512))
            for cc in range(4):
                ch = pc * 4 + cc
                for kc in range(8):
                    mm(modps[:, ch, :], w[:, kc, cc * 128:(cc + 1) * 128], scb[:, kc, :], kc == 0, kc == 7,
                       [wk, "scb"], [("ps", 6)])
        for s in range(2):
            tt(MODT[:, :, s], modps[:, :, s], VEC[:, l, 0:48], ALU.add, [("ps", 6), "VEC"], ["MODT"])
        for s in range(2):
            stt(PRM[:, 0, :, s], MODT[:, 8:16, s], 1.0, VEC[:, l, 48:56], ALU.add, ALU.mult, ["MODT", "VEC"], ["PRM"])
            stt(PRM[:, 1, :, s], MODT[:, 32:40, s], 1.0, VEC[:, l, 56:64], ALU.add, ALU.mult, ["MODT", "VEC"], ["PRM"])
            tt(PRM[:, 2, :, s], MODT[:, 16:24, s], VEC[:, l, 64:72], ALU.mult, ["MODT", "VEC"], ["PRM"])
            tt(PRM[:, 3, :, s], MODT[:, 40:48, s], VEC[:, l, 72:80], ALU.mult, ["MODT", "VEC"], ["PRM"])
        act(ES[:], VEC[:, l, 87:89], AF.Exp, ["VEC"], ["ES"])
        S.op("pool", lambda: nc.gpsimd.dma_start(out=SMW[:], in_=smw_d[l]), writes=["SMW"], dma=True)

    def gen_gg(which, s):
        S.phase = "gg"
        for c in range(8):
            dg = DIAG[c % 2]
            ts(dg[:], ident, PRM[:, 2 + which, c, s:s + 1], None, ALU.mult, None, ["PRM", "const"], [("diag", c % 2)])
            b = 4 + c // 4
            mm(PS(b)[:, (c % 4) * 128:(c % 4 + 1) * 128], ones, dg[:], True, True,
               [("diag", c % 2), "const"], [("ps", b)])
        for hh in range(2):
            act(GG[which][:, hh * 512:(hh + 1) * 512], PS(4 + hh), AF.Copy, [("ps", 4 + hh)], [("GG", which)])

    def norm_to_T(gi, which, mode="plain"):
        t0, NT, s = GROUPS[gi]
        ntt = NT // 128
        tile0 = t0 // 128
        S.phase = "norm"
        if mode != "cached":
            for i in range(ntt):
                if i % 2 == 0:
                    act(Tb(3)[:, 0:1024], X[:, tile0 + i, :], AF.Square, [("X", tile0 + i)], ["T3", "SS"],
                        accum_out=SS[:, i:i + 1])
                else:
                    xin = X[:, tile0 + i, :]
                    S.op("dve", lambda xin=xin, i=i: nc.vector.scalar_tensor_tensor(
                        out=Tb(2)[:, 0:1024], in0=xin, scalar=1.0, in1=xin, op0=ALU.mult, op1=ALU.mult,
                        accum_out=SS[:, i:i + 1]), reads=[("X", tile0 + i)], writes=["T2", "SS"])
            if mode == "store":
                act(LNV[:, 0:ntt], SS[:, 0:ntt], AF.Ln, ["SS", "EPSC"], ["LNV"], scale=1.0 / D, bias=EPSC[:, 0:1])
                act(RSTDALL[:, tile0:tile0 + ntt], LNV[:, 0:ntt], AF.Exp, ["LNV"], [("RSTDALL", gi)], scale=-0.5)
            else:
                rsqrt_small(ntt, 1.0 / D)
        if mode == "plain":
            rs, rk = RSTD, "RSTD"
            roff = 0
        else:
            rs, rk = RSTDALL, ("RSTDALL", gi)
            roff = tile0
        xsb = [(XS[:], "XS"), (Tb(0)[:, 0:1024], "T0")]
        for i in range(ntt):
            xb, xk = xsb[i % 2]
            ts(xb, X[:, tile0 + i, :], rs[:, roff + i:roff + i + 1], None, ALU.mult, None, [("X", tile0 + i), rk], [xk])
            for c in range(8):
                tr(TP4[:, c, i, :], xb[:, c * 128:(c + 1) * 128], [xk], [("ps", c // 2)])
        boff = 0 if which == 0 else 24
        for c in range(8):
            if c in (0, 1, 4, 5):
                act(HT[:, c, 0:NT], TP2[:, c * 512:c * 512 + NT], AF.Identity, [("ps", c // 2), "PRM", "MODT"], ["HT"],
                    scale=PRM[:, which, c, s:s + 1], bias=MODT[:, boff + c, s:s + 1])
            else:
                ts(HT[:, c, 0:NT], TP2[:, c * 512:c * 512 + NT], PRM[:, which, c, s:s + 1], MODT[:, boff + c, s:s + 1],
                   ALU.mult, ALU.add, [("ps", c // 2), "PRM", "MODT"], ["HT"])

    def load_tabs(gi):
        t0, NT, s = GROUPS[gi]
        dma_sp(TABg[:, :, 0:NT], tab_d[:, :, t0:t0 + NT].rearrange("a p t -> p a t"), [], ["TAB"])

    def proj(bank, wsl, wk, NT, M=128):
        for kc in range(8):
            mm(PS(bank)[0:M, 0:NT], wsl[:, kc, :], HT[:, kc, 0:NT], kc == 0, kc == 7, [wk, "HT"], [("ps", bank)])

    def rope_plain(dst, dkeys, bq, bqs, NT, rows=128, cosi=0):
        a = Tb(3)[0:rows, 0:NT]
        b = Tb(3)[0:rows, 544:544 + NT]
        tt(a, PS(bq)[0:rows, 0:NT], TABg[0:rows, cosi, 0:NT], ALU.mult, [("ps", bq), "TAB"], ["T3"])
        tt(b, PS(bqs)[0:rows, 0:NT], TABg[0:rows, cosi + 1, 0:NT], ALU.mult, [("ps", bqs), "TAB"], ["T3"])
        return a, b

    def normrope(dst, dkeys, bq, bqs, NT, l, gcol, extra_r=()):
        sq = XS[:, 0:NT]
        act(sq, PS(bq)[:, 0:NT], AF.Square, [("ps", bq)], ["XS"])
        chk(331)
        bs = nb()
        mm(PS(bs)[:, 0:NT], bd, sq, True, True, ["XS", "const"], [("ps", bs)])
        chk(332)
        rs = T[2][:, 0:NT]
        rsqrt_big(rs, PS(bs)[:, 0:NT], NT, 1.0 / 64, [("ps", bs)], ["T2"])
        chk(333)
        stt(T[0][:, 0:NT], PS(bq)[:, 0:NT], VEC[:, l, gcol:gcol + 1], TABg[:, 0, 0:NT], ALU.mult, ALU.mult,
            [("ps", bq), "VEC", "TAB"], ["T0"])
        chk(334)
        stt(T[1][:, 0:NT], PS(bqs)[:, 0:NT], VEC[:, l, gcol + 1:gcol + 2], TABg[:, 1, 0:NT], ALU.mult, ALU.mult,
            [("ps", bqs), "VEC", "TAB"], ["T1"])
        tt(T[0][:, 0:NT], T[0][:, 0:NT], T[1][:, 0:NT], ALU.add, ["T0", "T1"], ["T0"])
        tt(dst, T[0][:, 0:NT], rs, ALU.mult, ["T0", "T2"] + list(extra_r), dkeys)

    def pass1(l, gi):
        t0, NT, s = GROUPS[gi]
        ntt = NT // 128
        tile0 = t0 // 128
        norm_to_T(gi, 0, "store")
        chk(31)
        load_tabs(gi)
        S.phase = "p1proj"
        gk = gi
        wa, wk = wload(wp1_d[l][:, 0:512], (8, 512))
        b1, b2 = nb(), nb()
        proj(b1, wa[:, :, 0:128], wk, NT)
        proj(b2, wa[:, :, 128:256], wk, NT)
        chk(32)
        normrope(KTA[:, t0:t0 + NT], [("KTA", gk)], b1, b2, NT, l, 82)
        chk(33)
        b1, b2 = nb(), nb()
        proj(b1, wa[:, :, 256:384], wk, NT)
        proj(b2, wa[:, :, 384:512], wk, NT)
        a, b = rope_plain(None, None, b1, b2, NT)
        tt(KTC[:, t0:t0 + NT], a, b, ALU.add, ["T3"], [("KTC", gk)])
        chk(34)
        wb, wk = wload(wp1_d[l][:, 512:832], (8, 320))
        b1 = nb()
        proj(b1, wb[:, :, 0:128], wk, NT)
        sq = XS[:, 0:NT]
        act(sq, PS(b1)[:, 0:NT], AF.Square, [("ps", b1)], ["XS"])
        b2 = nb()
        mm(PS(b2)[:, 0:NT], ones, sq, True, True, ["XS", "const"], [("ps", b2)])
        rsqrt_big(T[2][:, 0:NT], PS(b2)[:, 0:NT], NT, 1.0 / 128, [("ps", b2)], ["T2"])
        kvn = Tb(0)[:, 0:NT]
        stt(kvn, PS(b1)[:, 0:NT], VEC[:, l, 86:87], T[2][:, 0:NT], ALU.mult, ALU.mult,
            [("ps", b1), "VEC", "T2"], ["T0"])
        for h in range(4):
            bh = nb()
            mm(PS(bh)[0:64, 0:NT], SMW[:, 1536 + h * 64:1536 + (h + 1) * 64], kvn, True, True, ["SMW", "T0"], [("ps", bh)])
            act(KTB[0:64, h, t0:t0 + NT], PS(bh)[0:64, 0:NT], AF.Copy, [("ps", bh)], [("KTB", gk)])
        for i in range(ntt):
            bv = nb()
            mm(PS(bv)[:, 0:256], kvn[:, i * 128:(i + 1) * 128], SMW[:, 1792:2048], True, True, ["SMW", "T0"], [("ps", bv)])
            cp(VB[:, tile0 + i, :], PS(bv)[:, 0:256], [("ps", bv)], [("VB", gk)])
        b1, b2 = nb(), nb()
        proj(b1, wb[:, :, 128:224], wk, NT, M=96)
        proj(b2, wb[:, :, 224:320], wk, NT, M=96)
        a, b = rope_plain(None, None, b1, b2, NT, rows=96, cosi=2)
        for h in range(4):
            tt(KTB[64:96, h, t0:t0 + NT], a[64:96, :], b[64:96, :], ALU.add, ["T3"], [("KTB", gk)])
        chk(35)
        wc, wk = wload(wp1_d[l][:, 832:1344], (8, 512))
        for i in range(ntt):
            bv = nb()
            for kc in range(8):
                mm(PS(bv)[:, 0:256], HT[:, kc, i * 128:(i + 1) * 128], wc[:, kc, 0:256], kc == 0, kc == 7,
                   [wk, "HT"], [("ps", bv)])
            act(VAC[:, tile0 + i, :], PS(bv)[:, 0:256], AF.Copy, [("ps", bv)], [("VAC", gk)])
        bu = nb()
        for c in range(2):
            for e, c0 in enumerate((0, NT - 8)):
                for kc in range(8):
                    mm(PS(bu)[:, c * 16 + e * 8:c * 16 + e * 8 + 8], wc[:, kc, 256 + c * 128:256 + (c + 1) * 128],
                       HT[:, kc, c0:c0 + 8], kc == 0, kc == 7, [wk, "HT"], [("ps", bu)])
        for c in range(2):
            cp(HALO[:, c, gi, :], PS(bu)[:, c * 16:(c + 1) * 16], [("ps", bu)], [("HALO", gi)])

    sctr = [0]
    pctr = [0]
    octr = [0]

    def attention(NT, qa, qb, ka, kb_, va, vb, rows, keylist, scale, otc, esink_col, qkeys, kkeyname, vkeyname):
        ob = 4
        O = PS(ob)
        Z = PS(ob + 1)
        n = len(keylist)
        ptsl = {}
        accv = Tb(0)[:, 0:1024].rearrange("p (a n) -> p a n", a=2)
        n_odd = n // 2
        last_even = ((n - 1) // 2) * 2

        def qk(idx):
            kt, c0, c1, masks = keylist[idx]
            pts = []
            for q_, k_ in ((qa, ka), (qb, kb_)):
                sbk = sctr[0] % 4
                sctr[0] += 1
                pi = pctr[0] % 4
                pctr[0] += 1
                mm(PS(sbk)[:, c0:c1], k_(kt), q_[:, c0:c1], True, True, qkeys + [gkey(kkeyname, kt), "Ra"], [("ps", sbk)])
                act(PT[pi][:, c0:c1], PS(sbk)[:, c0:c1], AF.Exp, [("ps", sbk), "Ra"], [("PT", pi)], scale=scale)
                for (m0, m1, mk) in masks:
                    tt(PT[pi][:, m0:m1], PT[pi][:, m0:m1], mk, ALU.mult, [("PT", pi), "const", "Ra"], [("PT", pi)])
                pts.append(pi)
            ptsl[idx] = pts

        def pv(idx):
            kt, c0, c1, masks = keylist[idx]
            first = idx == 0
            last = idx == n - 1
            pts = ptsl[idx]
            for hh, v_ in enumerate((va, vb)):
                pi = pts[hh]
                mm(O[hh * 64:(hh + 1) * 64, c0:c1], v_(kt), PT[pi][:, c0:c1], first, last,
                   [("PT", pi), gkey(vkeyname, kt), "Ra"], [("ps", ob)])
            for hh in range(2):
                pi = pts[hh]
                mm(Z[hh * 64:(hh + 1) * 64, c0:c1], ones[:, 0:64], PT[pi][:, c0:c1], first, last,
                   [("PT", pi), "const", "Ra"], [("ps", ob + 1)])

        qk(0)
        for idx in range(n):
            if idx + 1 < n:
                qk(idx + 1)
            pv(idx)
            yield
        rz = T[2][:, 0:NT]
        if esink_col is not None:
            ts(rz, Z[:, 0:NT], ES[:, esink_col:esink_col + 1], None, ALU.add, None, [("ps", ob + 1), "ES"], ["T2"])
            recip(rz, rz, ["T2"], ["T2"])
        else:
            recip(rz, Z[:, 0:NT], [("ps", ob + 1)], ["T2"])
        tt(OTg[:, otc, 0:NT], O[:, 0:NT], rz, ALU.mult, [("ps", ob), "T2", "Rb"], [("OT", otc)])
        yield

    def post_norm(gi, which):
        t0, NT, s = GROUPS[gi]
        ntt = NT // 128
        tile0 = t0 // 128
        for i in range(ntt):
            yv = PSALL[:, 2 * i * 512:(2 * i + 2) * 512]
            act(XS[:], yv, AF.Square, [("ps", 2 * i), ("ps", 2 * i + 1)], ["XS", ("SS", i)], accum_out=SS[:, i:i + 1])
            act(LNV[:, i:i + 1], SS[:, i:i + 1], AF.Ln, [("SS", i), "EPSC"], [("LNV", i)], scale=1.0 / D, bias=EPSC[:, 0:1])
            act(RSTD[:, i:i + 1], LNV[:, i:i + 1], AF.Exp, [("LNV", i)], [("RSTD", i)], scale=-0.5)
            for hh in range(2):
                ti = (i % 2) * 2 + hh
                tmp = T[ti][:, 0:512]
                stt(tmp, PS(2 * i + hh), RSTD[:, i:i + 1], GG[which][:, hh * 512:(hh + 1) * 512], ALU.mult, ALU.mult,
                    [("ps", 2 * i + hh), ("RSTD", i), ("GG", which)], [f"T{ti}"])
                xa = X[:, tile0 + i, hh * 512:(hh + 1) * 512]
                tt(xa, xa, tmp, ALU.add, [f"T{ti}", ("X", tile0 + i)], [("X", tile0 + i)])

    def pass2(l, gi):
        t0, NT, s = GROUPS[gi]
        ntt = NT // 128
        tile0 = t0 // 128
        norm_to_T(gi, 0, "cached")
        load_tabs(gi)
        fence(["Ra", "Rb"])
        S.phase = "qproj"
        wa, wk = wload(wp2_d[l][:, 0:512], (8, 512))
        for j in range(2):
            b1, b2 = nb(), nb()
            proj(b1, wa[:, :, j * 128:(j + 1) * 128], wk, NT)
            proj(b2, wa[:, :, 256 + j * 128:256 + (j + 1) * 128], wk, NT)
            normrope(QTA[:, j, 0:NT], [("QTA", j)], b1, b2, NT, l, 80, extra_r=["Ra"])
        wb, wkb = wload(wp2_d[l][:, 512:1024], (8, 512))
        wc, wkc = wload(wp2_d[l][:, 1024:1536], (8, 512))
        L = NT
        RS = WBS[0][:].rearrange("p a b -> p (a b)").bitcast(F32)
        rsk = ("wbs", 0)
        QBN = UE[:].rearrange("p a b -> p (a b)")[:, 0:1024].rearrange("p (a n) -> p a n", a=2)

        def filler():
            sq2 = XS[:].rearrange("p (a n) -> p a n", a=2)
            for c in range(2):
                proj(6 + c, wc[:, :, c * 128:(c + 1) * 128], wkc, NT)
                act(sq2[:, c, 0:NT], PS(6 + c)[:, 0:NT], AF.Square, [("ps", 6 + c)], ["XS"])
                cp(QBN[:, c, 0:NT], PS(6 + c)[:, 0:NT], [("ps", 6 + c)], ["UE"])
                yield "x"
            for c in range(2):
                mm(PS(6)[:, 0:NT], ones, sq2[:, c, 0:NT], c == 0, c == 1, ["XS", "const"], [("ps", 6)])
            rsqrt_big(RS[:, 0:NT], PS(6)[:, 0:NT], NT, 1.0 / 256, [("ps", 6)], [rsk])
            for c in range(2):
                stt(QBN[:, c, 0:NT], QBN[:, c, 0:NT], VEC[:, l, 84 + c:85 + c], RS[:, 0:NT], ALU.mult, ALU.mult,
                    ["UE", "VEC", rsk], ["UE"])
            yield "x"
            for h in range(4):
                for kc in range(2):
                    mm(PS(6)[0:96, 0:NT], SMW[:, kc * 384 + h * 96:kc * 384 + (h + 1) * 96], QBN[:, kc, 0:NT],
                       kc == 0, kc == 1, ["SMW", "UE"], [("ps", 6)])
                for kc in range(2):
                    mm(PS(7)[0:96, 0:NT], SMW[:, 768 + kc * 384 + h * 96:768 + kc * 384 + (h + 1) * 96],
                       QBN[:, kc, 0:NT], kc == 0, kc == 1, ["SMW", "UE"], [("ps", 7)])
                a, b = rope_plain(None, None, 6, 7, NT, rows=96, cosi=2)
                tt(QTB[0:96, h, 0:NT], a, b, ALU.add, ["T3", "Ra"], [("QTB", h)])
                yield "x"
            yield "qb_done"
            for j in range(2):
                proj(6, wb[:, :, j * 128:(j + 1) * 128], wkb, NT)
                yield "x"
                proj(7, wb[:, :, 256 + j * 128:256 + (j + 1) * 128], wkb, NT)
                a, b = rope_plain(None, None, 6, 7, NT)
                tt(QTC[:, j, 0:NT], a, b, ALU.add, ["T3", "Ra"], [("QTC", j)])
                yield "x"
            yield "qc_done"
            S.phase = "pool"
            for c in range(2):
                proj(6, wc[:, :, 256 + c * 128:256 + (c + 1) * 128], wkc, NT)
                if c == 0:
                    pass
                act(UE[:, c, 8:8 + L], PS(6)[:, 0:NT], AF.Copy, [("ps", 6)], ["UE"])
                if gi >= 2:
                    cp(UE[:, c, 0:8], HALO[:, c, gi - 1, 8:16], [("HALO", gi - 1)], ["UE"])
                else:
                    memset(UE[:, c, 0:8], 0.0, ["UE"])
                if 1 <= gi <= 3:
                    cp(UE[:, c, 8 + L:16 + L], HALO[:, c, gi + 1, 0:8], [("HALO", gi + 1)], ["UE"])
                else:
                    memset(UE[:, c, 8 + L:16 + L], 0.0, ["UE"])
                yield "x"
            tA = Tb(3)[:, 0:544]
            tB = Tb(3)[:, 544:1088]
            for c in range(2):
                for hb in range(2):
                    w_ = POOLW[2 * c + hb]
                    pr = slice(hb * 64, (hb + 1) * 64)
                    u = UE[pr, c, :]
                    if w_ == 2:
                        tt(tB[pr, 0:L], u[:, 7:7 + L], u[:, 8:8 + L], ALU.add, ["UE"], ["T3"])
                    elif w_ == 4:
                        tt(tA[pr, 0:L + 2], u[:, 6:8 + L], u[:, 7:9 + L], ALU.add, ["UE"], ["T3"])
                        tt(tB[pr, 0:L], tA[pr, 0:L], tA[pr, 2:L + 2], ALU.add, ["T3"], ["T3"])
                    elif w_ == 8:
                        tt(tA[pr, 0:L + 6], u[:, 4:10 + L], u[:, 5:11 + L], ALU.add, ["UE"], ["T3"])
                        tt(tB[pr, 0:L + 4], tA[pr, 0:L + 4], tA[pr, 2:L + 6], ALU.add, ["T3"], ["T3"])
                        tt(tA[pr, 0:L], tB[pr, 0:L], tB[pr, 4:L + 4], ALU.add, ["T3"], ["T3"])
                        cp(tB[pr, 0:L], tA[pr, 0:L], ["T3"], ["T3"])
                    else:
                        tt(tA[pr, 0:L + 14], u[:, 0:14 + L], u[:, 1:15 + L], ALU.add, ["UE"], ["T3"])
                        tt(tB[pr, 0:L + 12], tA[pr, 0:L + 12], tA[pr, 2:L + 14], ALU.add, ["T3"], ["T3"])
                        tt(tA[pr, 0:L + 8], tB[pr, 0:L + 8], tB[pr, 4:L + 12], ALU.add, ["T3"], ["T3"])
                        tt(tB[pr, 0:L], tA[pr, 0:L], tA[pr, 8:L + 8], ALU.add, ["T3"], ["T3"])
                    yield "x"
                tt(tA[:, 0:L], tB[:, 0:L], TABg[:, 4 + c, 0:L], ALU.mult, ["T3", "TAB"], ["T3"])
                pm = XS[:, 0:L]
                tt(pm, tA[:, 0:L], UE[:, c, 8:8 + L], ALU.subtract, ["T3", "UE"], ["XS"])
                mm(PS(7)[:, 0:L], SMW[:, 2048 + c * 128:2048 + (c + 1) * 128], pm, True, True, ["XS", "SMW"], [("ps", 7)])
                act(OTg[:, 6 + c, 0:L], PS(7)[:, 0:L], AF.Identity, [("ps", 7), "VEC", "Rb"], [("OT", 6 + c)],
                    scale=VEC[:, l, 89 + c:90 + c])
                yield "x"
            S.phase = "attn"
            yield "pool_done"

        S.phase = "attn"
        if s == 1:
            keys_full = [(kt, 0, NT, []) for kt in range(2)]
        else:
            keys_full = [(kt, 0, NT, []) for kt in range(18)]
        if s == 1:
            keys_c = [(kt, 0, NT, []) for kt in range(2)]
        else:
            qb0 = 4 * (gi - 1)
            keys_c = [(kt, 0, NT, []) for kt in range(2)]
            for kb in range(max(0, qb0 - 1), min(15, qb0 + 4) + 1):
                ilo = max(0, kb - 1 - qb0)
                ihi = min(3, kb + 1 - qb0)
                masks = []
                for i in range(ilo, ihi + 1):
                    qblk = qb0 + i
                    if kb == qblk - 1:
                        masks.append((i * 128, (i + 1) * 128, maskL))
                    elif kb == qblk + 1:
                        masks.append((i * 128, (i + 1) * 128, maskU))
                keys_c.append((2 + kb, ilo * 128, (ihi + 1) * 128, masks))

        def att_a(j):
            return attention(NT, QTA[0:64, j, :], QTA[64:128, j, :],
                             lambda kt: KTA[0:64, kt * 128:(kt + 1) * 128], lambda kt: KTA[64:128, kt * 128:(kt + 1) * 128],
                             lambda kt: VAC[:, kt, 0:64], lambda kt: VAC[:, kt, 64:128],
                             64, keys_full, 0.125, j, None, [("QTA", j)], "KTA", "VAC")

        def att_b(j):
            ha, hb_ = 2 * j, 2 * j + 1
            return attention(NT, QTB[0:96, ha, :], QTB[0:96, hb_, :],
                             lambda kt, h=ha: KTB[0:96, h, kt * 128:(kt + 1) * 128],
                             lambda kt, h=hb_: KTB[0:96, h, kt * 128:(kt + 1) * 128],
                             lambda kt, h=ha: VB[:, kt, h * 64:(h + 1) * 64],
                             lambda kt, h=hb_: VB[:, kt, h * 64:(h + 1) * 64],
                             96, keys_full, 96 ** -0.5, 2 + j, None, [("QTB", ha), ("QTB", hb_)], "KTB", "VB")

        def att_c(j):
            return attention(NT, QTC[0:64, j, :], QTC[64:128, j, :],
                             lambda kt: KTC[0:64, kt * 128:(kt + 1) * 128], lambda kt: KTC[64:128, kt * 128:(kt + 1) * 128],
                             lambda kt: VAC[:, kt, 128:192], lambda kt: VAC[:, kt, 192:256],
                             64, keys_c, 0.125, 4 + j, j, [("QTC", j)], "KTC", "VAC")

        fg = filler()
        fstate = {"qb_done": False, "qc_done": False, "pool_done": False}

        def fstep():
            try:
                r = next(fg)
                if r in fstate:
                    fstate[r] = True
                return True
            except StopIteration:
                return False

        def drain(flag):
            while not fstate[flag]:
                if not fstep():
                    break

        for kind, need in (("a", None), ("b", "qb_done"), ("c", "qc_done")):
            if need is not None:
                drain(need)
            for j in range(2):
                g = {"a": att_a, "b": att_b, "c": att_c}[kind](j)
                for _ in g:
                    fstep()
        drain("pool_done")
        S.phase = "gate"
        fence(["Ra"])
        otk = [("OT", i) for i in range(8)]
        for dc in range(8):
            wg, wk = wload(wp2_d[l][:, 1536 + dc * 512:1536 + (dc + 1) * 512], (8, 512))
            wbp, wbk = wbload(wbp_d[l][dc])
            for k in range(4):
                gb = k % 2
                pb = 2 + k % 2
                for kc in range(8):
                    mm(PS(gb)[:, 0:NT], wg[:, kc, k * 128:(k + 1) * 128], HT[:, kc, 0:NT], kc == 0, kc == 7,
                       [wk, "HT"], [("ps", gb)])
                for kc2 in range(2):
                    mm(PS(pb)[:, 0:NT], wbp[:, 2 * k + kc2, :], OTg[:, 2 * k + kc2, 0:NT], kc2 == 0, kc2 == 1,
                       [wbk, ("OT", 2 * k + kc2), "Rb"], [("ps", pb)])
                sg = T[k % 2][:, 0:NT]
                act(sg, PS(gb)[:, 0:NT], AF.Sigmoid, [("ps", gb)], [f"T{k % 2}"])
                if k == 0:
                    tt(T[2][:, 0:NT], PS(pb)[:, 0:NT], sg, ALU.mult, [("ps", pb), f"T{k % 2}"], ["T2"])
                else:
                    tt(T[3][:, 0:NT], PS(pb)[:, 0:NT], sg, ALU.mult, [("ps", pb), f"T{k % 2}"], ["T3"])
                    if k < 3:
                        tt(T[2][:, 0:NT], T[2][:, 0:NT], T[3][:, 0:NT], ALU.add, ["T2", "T3"], ["T2"])
                    else:
                        tt(sTg[:, dc, 0:NT], T[2][:, 0:NT], T[3][:, 0:NT], ALU.add, ["T2", "T3", "Ra"], [("sT", dc)])
        S.phase = "wout"
        for half in range(2):
            wo, wk = wload(wo_d[l][half * 512:(half + 1) * 512, :], (4, 1024))
            for kcl in range(4):
                kc = half * 4 + kcl
                for i in range(ntt):
                    for hh in range(2):
                        mm(PS(2 * i + hh), sTg[:, kc, i * 128:(i + 1) * 128], wo[:, kcl, hh * 512:(hh + 1) * 512],
                           kc == 0, kc == 7, [wk, ("sT", kc), "Ra"], [("ps", 2 * i + hh)])
        S.phase = "postnorm"
        post_norm(gi, 0)
        norm_to_T(gi, 1)
        fence(["Ra", "Rb"])
        S.phase = "ffn13"
        for jp in range(11):
            w, wk = wload(w13_d[l][:, jp * 512:(jp + 1) * 512], (8, 512))
            for jj in range(2):
                j = 2 * jp + jj
                b1 = 2 * (j % 2)
                b3 = b1 + 1
                for kc in range(8):
                    mm(PS(b1)[:, 0:NT], w[:, kc, jj * 256:jj * 256 + 128], HT[:, kc, 0:NT], kc == 0, kc == 7,
                       [wk, "HT"], [("ps", b1)])
                for kc in range(8):
                    mm(PS(b3)[:, 0:NT], w[:, kc, jj * 256 + 128:jj * 256 + 256], HT[:, kc, 0:NT], kc == 0, kc == 7,
                       [wk, "HT"], [("ps", b3)])
                sl = T[j % 2][:, 0:NT]
                act(sl, PS(b1)[:, 0:NT], AF.Silu, [("ps", b1)], [f"T{j % 2}"])
                tt(gT[:, j, 0:NT], PS(b3)[:, 0:NT], sl, ALU.mult, [("ps", b3), f"T{j % 2}", "Ra", "Rb"], [("gT", j)])
        S.phase = "ffn2"
        for pc in range(6):
            nj = 4 if pc < 5 else 2
            w2, wk = wload(w2_d[l][pc * 512:pc * 512 + nj * 128, :], (nj, 1024))
            for jl in range(nj):
                j = pc * 4 + jl
                for i in range(ntt):
                    for hh in range(2):
                        mm(PS(2 * i + hh), gT[:, j, i * 128:(i + 1) * 128], w2[:, jl, hh * 512:(hh + 1) * 512],
                           j == 0, j == NJ - 1, [wk, ("gT", j), "Ra", "Rb"], [("ps", 2 * i + hh)])
        S.phase = "postnorm"
        post_norm(gi, 1)

    def forward():
        if dbg == 1:
            return
        for l in range(depth):
            with_ctx = l < depth - 1
            p0(l)
            if dbg == 2:
                return
            for gi in range(5):
                pass1(l, gi)
                if dbg == 3:
                    return
            if dbg == 4:
                return
            if with_ctx:
                gen_gg(0, 1)
                gen_gg(1, 1)
                pass2(l, 0)
            gen_gg(0, 0)
            gen_gg(1, 0)
            if dbg == 5:
                return
            for gi in range(1, 5):
                pass2(l, gi)
                if dbg == 6:
                    return

    try:
        forward()
    except StopBuild:
        pass
    for i in range(4):
        dma_sp(out_d[512 * i:512 * (i + 1), :].rearrange("(t p) d -> p t d", p=128), X[:, 2 + 4 * i:6 + 4 * i, :],
               [("X", 2 + 4 * i + k) for k in range(4)], [])
    if not emit:
        return S
    S.emit()
    return nc


def _swap_heads(w, hd):
    k, n = w.shape
    w4 = w.reshape(k, n // hd, 2, hd // 2)
    return np.ascontiguousarray(w4[:, :, ::-1, :]).reshape(k, n)


def _consts():
    bf = ml_dtypes.bfloat16
    cb = np.zeros((128, 640), np.float32)
    cb[:, 0:128] = np.eye(128)
    cb[:, 128:256] = 1.0
    cb[0:64, 256:320] = 1.0
    cb[64:128, 320:384] = 1.0
    j = np.arange(128)[:, None]
    i = np.arange(128)[None, :]
    cb[:, 384:512] = (j >= i)
    cb[:, 512:640] = (j <= i)
    t = np.arange(SEQ)
    row = (t // 64).astype(np.float32)
    col = (t % 64).astype(np.float32)

    def angles(rot_dim):
        nf = rot_dim // 4
        inv = (np.float32(10000.0) ** (-np.arange(nf, dtype=np.float32) / nf)).astype(np.float32)
        ang = np.concatenate([row[:, None] * inv, col[:, None] * inv], axis=-1).astype(np.float32)
        return np.cos(ang).astype(np.float32), np.sin(ang).astype(np.float32)

    c64, s64 = angles(64)
    c32, s32 = angles(32)
    tab = np.zeros((6, 128, NTOK), np.float32)
    tab[0, :, :CTX] = 1.0
    tab[2, :, :] = 1.0
    for p in range(128):
        d = p % 64
        f = d % 32
        tab[0, p, CTX:] = c64[:, f]
        tab[1, p, CTX:] = (-s64[:, f]) if d < 32 else s64[:, f]
    for p in range(64, 96):
        r = p - 64
        f = r % 16
        tab[2, p, CTX:] = c32[:, f]
        tab[3, p, CTX:] = (-s32[:, f]) if r < 16 else s32[:, f]
    for c in range(2):
        for hb in range(2):
            w = POOLW[2 * c + hb]
            lo = w // 2
            hi = w - lo - 1
            for (o0, Tn) in ((0, CTX), (CTX, SEQ)):
                pos = np.arange(Tn)
                st = np.clip(pos - lo, 0, Tn)
                en = np.clip(pos + hi + 1, 0, Tn)
                tab[4 + c, hb * 64:(hb + 1) * 64, o0:o0 + Tn] = (1.0 / (en - st).astype(np.float32))[None, :]
    return cb.astype(bf), tab.astype(bf)


def prep_shared(inp):
    f = lambda k: np.asarray(inp[k], np.float32)
    w_in = f("w_in")
    L = w_in.shape[0]
    o = np.cumsum([0, 256, 128, 128, 256, 128, 32, 256, 128, 128, 256, 4096])
    qa, ka, va, qba, kvba, kbr, qc, kc, vc, u, gl = [w_in[:, :, o[i]:o[i + 1]] for i in range(11)]
    hperm = np.concatenate([np.arange(0, 64), np.arange(128, 192), np.arange(64, 128), np.arange(192, 256)])
    wp1 = np.zeros((L, D, 1344), np.float32)
    wp2 = np.zeros((L, D, 5632), np.float32)
    wbp = np.zeros((L, 8, D, 128), np.float32)
    smw = np.zeros((L, 128, 2304), np.float32)
    vec = np.zeros((128, L, NV), np.float32)
    w13 = np.zeros((L, D, NJ * 256), np.float32)
    w_b = f("w_branch")
    for l in range(L):
        wp1[l, :, 0:128] = ka[l]
        wp1[l, :, 128:256] = _swap_heads(ka[l], 64)
        wp1[l, :, 256:384] = kc[l]
        wp1[l, :, 384:512] = _swap_heads(kc[l], 64)
        wp1[l, :, 512:640] = kvba[l]
        wp1[l, :, 640 + 64:640 + 96] = kbr[l]
        wp1[l, :, 736 + 64:736 + 96] = _swap_heads(kbr[l], 32)
        wp1[l, :, 832:960] = va[l]
        wp1[l, :, 960:1088] = vc[l]
        wp1[l, :, 1088:1344] = u[l]
        qap = qa[l][:, hperm]
        wp2[l, :, 0:256] = qap
        wp2[l, :, 256:512] = _swap_heads(qap, 64)
        qcp = qc[l][:, hperm]
        wp2[l, :, 512:768] = qcp
        wp2[l, :, 768:1024] = _swap_heads(qcp, 64)
        wp2[l, :, 1024:1280] = qba[l]
        wp2[l, :, 1280:1536] = u[l]
        g4 = gl[l].reshape(D, 4, 8, 128)
        wp2[l, :, 1536:] = np.ascontiguousarray(g4.transpose(0, 2, 1, 3)).reshape(D, 4096)
        for k in range(4):
            wk_ = w_b[l, k]
            if k in (0, 2):
                wk_ = wk_[hperm, :]
            for dc in range(8):
                wbp[l, dc, k * 256:(k + 1) * 256, :] = wk_[:, dc * 128:(dc + 1) * 128]
        wqb = f("b_w_qb")[l]
        wqbs = np.zeros_like(wqb)
        for h in range(4):
            wqbs[:, h * 96 + 64:h * 96 + 96] = _swap_heads(wqb[:, h * 96 + 64:h * 96 + 96], 32)
        smw[l, :, 0:768] = wqb.reshape(2, 128, 384).transpose(1, 0, 2).reshape(128, 768)
        smw[l, :, 768:1536] = wqbs.reshape(2, 128, 384).transpose(1, 0, 2).reshape(128, 768)
        wkvb = f("b_w_kvb")[l].reshape(128, 4, 128)
        smw[l, :, 1536:1792] = wkvb[:, :, 0:64].reshape(128, 256)
        smw[l, :, 1792:2048] = wkvb[:, :, 64:128].reshape(128, 256)
        wpool = f("d_w_pool")[l]
        for c in range(2):
            for hb in range(2):
                smw[l, hb * 64:(hb + 1) * 64, 2048 + c * 128 + hb * 64:2048 + c * 128 + (hb + 1) * 64] = wpool[2 * c + hb]
        vec[:, l, 0:48] = f("ada_b")[l].reshape(48, 128).T
        vec[:, l, 48:56] = f("mix_pre_g")[l].reshape(8, 128).T
        vec[:, l, 56:64] = f("ffn_pre_g")[l].reshape(8, 128).T
        vec[:, l, 64:72] = f("mix_post_g")[l].reshape(8, 128).T
        vec[:, l, 72:80] = f("ffn_post_g")[l].reshape(8, 128).T
        for cbase, g in ((80, f("a_qn_g")[l]), (82, f("a_kn_g")[l])):
            vec[:, l, cbase] = np.tile(g, 2)
            vec[:, l, cbase + 1] = np.tile(np.concatenate([g[32:], g[:32]]), 2)
        vec[:, l, 84:86] = f("b_qa_g")[l].reshape(2, 128).T
        vec[:, l, 86] = f("b_kva_g")[l]
        sk = f("c_sink")[l]
        vec[0:64, l, 87] = sk[0]
        vec[64:128, l, 87] = sk[2]
        vec[0:64, l, 88] = sk[1]
        vec[64:128, l, 88] = sk[3]
        vec[:, l, 89:91] = f("d_scale")[l].reshape(2, 128).T
        w1 = f("w_ffn1")[l].reshape(D, NJ, 128)
        w3 = f("w_ffn3")[l].reshape(D, NJ, 128)
        w13[l] = np.stack([w1, w3], axis=2).reshape(D, NJ * 256)
    cb, tab = _consts()
    return {
        "vec": np.ascontiguousarray(vec.reshape(128, L * NV)),
        "constb": cb, "tab": tab,
        "ada": f("ada_w"), "wp1": wp1, "wp2": wp2, "wbp": wbp, "wo": f("w_out"),
        "w13": w13, "w2": f("w_ffn2"), "smw": smw,
    }


def prep_core(inp, b):
    c = np.asarray(inp["c"], np.float32)[b]
    cc = np.asarray(inp["c_ctx"], np.float32)
    cT = np.stack([c.reshape(8, 128).T, cc.reshape(8, 128).T], axis=2).reshape(128, 16)
    return {
        "x": np.ascontiguousarray(np.asarray(inp["x"], np.float32)[b]),
        "ctx": np.ascontiguousarray(np.asarray(inp["ctx"], np.float32)[b]),
        "cT": np.ascontiguousarray(cT),
    }


def kernel(**inputs):
    nb_ = inputs["x"].shape[0]
    shared = prep_shared(inputs)
    nc = build_program(depth=4)
    in_maps = []
    for b in range(nb_):
        m = dict(shared)
        m.update(prep_core(inputs, b))
        in_maps.append(m)
    res = run_bass_kernel_spmd(nc, in_maps, core_ids=list(range(nb_)))
    return np.stack([np.asarray(r["out"], np.float32) for r in res.results], axis=0)
```
